# Optimizing a Trainium2 kernel written in Bass

```python
import jax, jax.numpy as jnp
from jax import lax
import numpy as np

D_MODEL = 1024
BATCH = 16
SEQ = 2048
DEPTH = 2

GRID_W = 64
CTX_LEN = 256
HEAD_DIM = 64
ATT_WIDTH = D_MODEL // 2
ML_WIDTH = D_MODEL // 4
POOL_WIDTH = D_MODEL - ATT_WIDTH - ML_WIDTH
MIX_WIDTH = ATT_WIDTH + ML_WIDTH + POOL_WIDTH
ATT_HEADS = ATT_WIDTH // HEAD_DIM
ATT_KV_HEADS = ATT_HEADS // 4
KV_WIDTH = ATT_KV_HEADS * HEAD_DIM
ML_HEADS = ML_WIDTH // HEAD_DIM
POOL_WINDOWS = (2, 4, 8, 16)
POOL_GROUPS = len(POOL_WINDOWS)
POOL_GROUP_DIM = POOL_WIDTH // POOL_GROUPS
N_GATES = 4 * ML_HEADS
IN_SPLITS = (ATT_WIDTH, KV_WIDTH, KV_WIDTH, ML_WIDTH, ML_WIDTH, ML_WIDTH, ML_WIDTH, N_GATES, POOL_WIDTH)
IN_WIDTH = sum(IN_SPLITS)
IN_OFFSETS = tuple(int(o) for o in np.cumsum(IN_SPLITS)[:-1])
D_FF = -(-8 * D_MODEL // (3 * 256)) * 256
Q_BLOCK = 128
CHUNK = 128
ROPE_THETA = 10000.0
EPS = 1e-6

kernel_name = 'hymba_style_attn_mlstm_pool_prefix_dit'

F32 = jnp.float32


def rms_norm(x, g):
    xf = x.astype(F32)
    y = xf * lax.rsqrt(jnp.mean(xf * xf, -1, keepdims=True) + EPS)
    return (y * g.astype(F32)).astype(x.dtype)


def axial_rope_tables(rows):
    half = HEAD_DIM // 2
    inv_freq = 1.0 / (ROPE_THETA ** (jnp.arange(0, half, 2, dtype=F32) / half))
    row = jnp.repeat(jnp.arange(rows, dtype=F32), GRID_W)
    col = jnp.tile(jnp.arange(GRID_W, dtype=F32), rows)
    a_row = row[:, None] * inv_freq
    a_col = col[:, None] * inv_freq
    ang = jnp.concatenate([a_row, a_row, a_col, a_col], -1)
    return jnp.cos(ang), jnp.sin(ang)


def apply_rope(x, cos, sin):
    q = HEAD_DIM // 4
    xr = x.reshape(x.shape[:-1] + (2, 2, q))
    rot = jnp.stack([-xr[..., 1, :], xr[..., 0, :]], -2).reshape(x.shape)
    out = x.astype(F32) * cos[:, None, :] + rot.astype(F32) * sin[:, None, :]
    return out.astype(x.dtype)


def block_attention(q, k, v):
    B, T, Hq, d = q.shape
    Hk = k.shape[2]
    G = Hq // Hk
    nb = T // Q_BLOCK
    qb = q.reshape(B, nb, Q_BLOCK, Hk, G, d).swapaxes(0, 1)
    scale = d ** -0.5

    def one_block(qi):
        s = jnp.einsum('bqhgd,bkhd->bhgqk', qi, k, preferred_element_type=F32) * scale
        p = jax.nn.softmax(s, -1).astype(v.dtype)
        return jnp.einsum('bhgqk,bkhd->bqhgd', p, v)

    o = lax.map(one_block, qb)
    return o.swapaxes(0, 1).reshape(B, T, Hq * d)


def mlstm_scan(q, k, v, log_i, log_f, state):
    B, H, T, d = q.shape
    nc = T // CHUNK

    def split(a):
        a = a.astype(F32)
        return jnp.moveaxis(a.reshape(a.shape[:2] + (nc, CHUNK) + a.shape[3:]), 2, 0)

    xs = (split(q), split(k * d ** -0.5), split(v), split(log_i), split(log_f))
    tril = jnp.tril(jnp.ones((CHUNK, CHUNK), bool))

    def step(carry, inp):
        C, n, m = carry
        qc, kc, vc, li, lf = inp
        b = jnp.cumsum(lf, -1)
        dmat = jnp.where(tril, b[..., :, None] - b[..., None, :] + li[..., None, :], -jnp.inf)
        inter = b + m[..., None]
        m_t = jnp.maximum(inter, jnp.max(dmat, -1))
        w_inter = jnp.exp(inter - m_t)
        s = jnp.einsum('bhtd,bhsd->bhts', qc, kc) * jnp.exp(dmat - m_t[..., None])
        num = w_inter[..., None] * jnp.einsum('bhvk,bhtk->bhtv', C, qc) + jnp.einsum('bhts,bhsv->bhtv', s, vc)
        den = w_inter * jnp.einsum('bhk,bhtk->bht', n, qc) + jnp.sum(s, -1)
        h = num / jnp.maximum(jnp.abs(den), jnp.exp(-m_t))[..., None]
        b_last = b[..., -1]
        g = b_last[..., None] - b + li
        m_new = jnp.maximum(b_last + m, jnp.max(g, -1))
        a = jnp.exp(b_last + m - m_new)
        wk = jnp.exp(g - m_new[..., None])
        C_new = a[..., None, None] * C + jnp.einsum('bhs,bhsv,bhsk->bhvk', wk, vc, kc)
        n_new = a[..., None] * n + jnp.einsum('bhs,bhsk->bhk', wk, kc)
        return (C_new, n_new, m_new), h

    state, hs = lax.scan(step, state, xs)
    return jnp.moveaxis(hs, 0, 2).reshape(B, H, T, d), state


def bidir_mlstm(q_c, k_c, v_c, g_c, q_l, k_l, v_l, g_l):
    B, H, _, d = q_l.shape
    zero = (jnp.zeros((B, H, d, d), F32), jnp.zeros((B, H, d), F32), jnp.zeros((B, H), F32))
    outs_c, outs_l = [], []
    for direc in range(2):
        rev = (lambda a: jnp.flip(a, 2)) if direc else (lambda a: a)
        li_c, lf_c = g_c[2 * direc], jax.nn.log_sigmoid(g_c[2 * direc + 1])
        li_l, lf_l = g_l[2 * direc], jax.nn.log_sigmoid(g_l[2 * direc + 1])
        h_c, st = mlstm_scan(rev(q_c), rev(k_c), rev(v_c), rev(li_c), rev(lf_c), zero)
        h_l, _ = mlstm_scan(rev(q_l), rev(k_l), rev(v_l), rev(li_l), rev(lf_l), st)
        outs_c.append(rev(h_c))
        outs_l.append(rev(h_l))
    return outs_c[0] + outs_c[1], outs_l[0] + outs_l[1]


def mlstm_readout(h, o_pre, ml_norm):
    B, H, T, d = h.shape
    hn = rms_norm(h.transpose(0, 2, 1, 3), ml_norm.reshape(H, d)).reshape(B, T, H * d)
    return (hn * jax.nn.sigmoid(o_pre.astype(F32))).astype(o_pre.dtype)


def pool_mix(u, pool_w, pool_scale):
    B, T, _ = u.shape
    uf = u.astype(F32)
    cs = jnp.concatenate([jnp.zeros((B, 1, POOL_WIDTH), F32), jnp.cumsum(uf, 1)], 1)
    t = jnp.arange(T)
    outs = []
    for gi, w in enumerate(POOL_WINDOWS):
        lo = jnp.maximum(t - w // 2, 0)
        hi = jnp.minimum(t + w // 2, T)
        sl = slice(gi * POOL_GROUP_DIM, (gi + 1) * POOL_GROUP_DIM)
        csg = cs[:, :, sl]
        mean = (csg[:, hi] - csg[:, lo]) / (hi - lo).astype(F32)[None, :, None]
        outs.append(mean - uf[:, :, sl])
    p = jnp.stack(outs, 2)
    y = jnp.einsum('btgi,gio->btgo', p, pool_w.astype(F32)).reshape(B, T, POOL_WIDTH)
    return (y * pool_scale.astype(F32)).astype(u.dtype)


def swiglu(h, w_g, w_u, w_d):
    return (jax.nn.silu(h @ w_g) * (h @ w_u)) @ w_d


def hybrid_layer(xl, xc, c, c_ctx, w_ada, b_ada, norm_mix, w_in, b_gates, q_norm, k_norm, ml_norm,
                 pool_w, pool_scale, w_out, norm_ffn, w_ffn_gate, w_ffn_up, w_ffn_down, cos, sin, with_ctx_out):
    B, T, _ = xl.shape
    Tc = xc.shape[1]
    mod_l = (jax.nn.silu(c) @ w_ada + b_ada)[:, None, :]
    mod_c = jax.nn.silu(c_ctx) @ w_ada + b_ada
    sh1_l, sc1_l, g1_l, sh2_l, sc2_l, g2_l = jnp.split(mod_l, 6, -1)
    sh1_c, sc1_c, g1_c, sh2_c, sc2_c, g2_c = jnp.split(mod_c, 6, -1)

    hl = rms_norm(xl, norm_mix) * (1.0 + sc1_l) + sh1_l
    hc = rms_norm(xc, norm_mix) * (1.0 + sc1_c) + sh1_c
    aq_l, ak_l, av_l, mq_l, mk_l, mv_l, mo_l, mg_l, pp_l = jnp.split(hl @ w_in, IN_OFFSETS, -1)
    aq_c, ak_c, av_c, mq_c, mk_c, mv_c, mo_c, mg_c, pp_c = jnp.split(hc @ w_in, IN_OFFSETS, -1)

    q_l = apply_rope(rms_norm(aq_l.reshape(B, T, ATT_HEADS, HEAD_DIM), q_norm), cos, sin)
    k_l = apply_rope(rms_norm(ak_l.reshape(B, T, ATT_KV_HEADS, HEAD_DIM), k_norm), cos, sin)
    v_l = av_l.reshape(B, T, ATT_KV_HEADS, HEAD_DIM)
    k_c = rms_norm(ak_c.reshape(B, Tc, ATT_KV_HEADS, HEAD_DIM), k_norm)
    v_c = av_c.reshape(B, Tc, ATT_KV_HEADS, HEAD_DIM)
    att_l = block_attention(q_l, jnp.concatenate([k_l, k_c], 1), jnp.concatenate([v_l, v_c], 1))

    def ml_heads(a, n):
        return a.reshape(B, n, ML_HEADS, HEAD_DIM).transpose(0, 2, 1, 3)

    def ml_gates(gp, n):
        return (gp + b_gates).astype(F32).reshape(B, n, 4, ML_HEADS).transpose(2, 0, 3, 1)

    hm_c, hm_l = bidir_mlstm(ml_heads(mq_c, Tc), ml_heads(mk_c, Tc), ml_heads(mv_c, Tc), ml_gates(mg_c, Tc),
                             ml_heads(mq_l, T), ml_heads(mk_l, T), ml_heads(mv_l, T), ml_gates(mg_l, T))
    ml_l = mlstm_readout(hm_l, mo_l, ml_norm)

    pool_l = pool_mix(pp_l, pool_w, pool_scale)

    xl = xl + g1_l * (jnp.concatenate([att_l, ml_l, pool_l], -1) @ w_out)
    h2l = rms_norm(xl, norm_ffn) * (1.0 + sc2_l) + sh2_l
    xl = xl + g2_l * swiglu(h2l, w_ffn_gate, w_ffn_up, w_ffn_down)

    if with_ctx_out:
        q_c = rms_norm(aq_c.reshape(B, Tc, ATT_HEADS, HEAD_DIM), q_norm)
        att_c = block_attention(q_c, k_c, v_c)
        ml_c = mlstm_readout(hm_c, mo_c, ml_norm)
        pool_c = pool_mix(pp_c, pool_w, pool_scale)
        xc = xc + g1_c * (jnp.concatenate([att_c, ml_c, pool_c], -1) @ w_out)
        h2c = rms_norm(xc, norm_ffn) * (1.0 + sc2_c) + sh2_c
        xc = xc + g2_c * swiglu(h2c, w_ffn_gate, w_ffn_up, w_ffn_down)
    return xl, xc


def setup_inputs(seed: int = 0) -> dict:
    key = jax.random.key(seed)
    ks = jax.random.split(key, 22)

    def dense(k, shape, fan_in, mult=1.0):
        return mult * jax.random.normal(k, shape, F32) * fan_in ** -0.5

    def gain(k, shape):
        return 1.0 + 0.05 * jax.random.normal(k, shape, F32)

    f_bias = jnp.linspace(3.0, 6.0, ML_HEADS, dtype=F32)
    gate_base = jnp.stack([jnp.zeros_like(f_bias), f_bias, jnp.zeros_like(f_bias), f_bias])
    b_gates = (gate_base[None] + 0.1 * jax.random.normal(ks[8], (DEPTH, 4, ML_HEADS), F32)).reshape(DEPTH, N_GATES)
    return {
        'x': jax.random.normal(ks[0], (BATCH, SEQ, D_MODEL), F32),
        'c': jax.random.normal(ks[1], (BATCH, D_MODEL), F32),
        'ctx': jax.random.normal(ks[2], (BATCH, CTX_LEN, D_MODEL), F32),
        'c_ctx': jax.random.normal(ks[3], (D_MODEL,), F32),
        'w_ada': dense(ks[4], (DEPTH, D_MODEL, 6 * D_MODEL), D_MODEL, 0.5),
        'b_ada': 0.02 * jax.random.normal(ks[5], (DEPTH, 6 * D_MODEL), F32),
        'norm_mix': gain(ks[6], (DEPTH, D_MODEL)),
        'w_in': dense(ks[7], (DEPTH, D_MODEL, IN_WIDTH), D_MODEL),
        'b_gates': b_gates,
        'q_norm': gain(ks[9], (DEPTH, HEAD_DIM)),
        'k_norm': gain(ks[10], (DEPTH, HEAD_DIM)),
        'ml_norm': gain(ks[11], (DEPTH, ML_WIDTH)),
        'pool_w': dense(ks[12], (DEPTH, POOL_GROUPS, POOL_GROUP_DIM, POOL_GROUP_DIM), POOL_GROUP_DIM),
        'pool_scale': 1.0 + 0.1 * jax.random.normal(ks[13], (DEPTH, POOL_WIDTH), F32),
        'w_out': dense(ks[14], (DEPTH, MIX_WIDTH, D_MODEL), MIX_WIDTH),
        'norm_ffn': gain(ks[15], (DEPTH, D_MODEL)),
        'w_ffn_gate': dense(ks[16], (DEPTH, D_MODEL, D_FF), D_MODEL),
        'w_ffn_up': dense(ks[17], (DEPTH, D_MODEL, D_FF), D_MODEL),
        'w_ffn_down': dense(ks[18], (DEPTH, D_FF, D_MODEL), D_FF),
        'final_norm': gain(ks[19], (D_MODEL,)),
    }


def reference(x, c, ctx, c_ctx, w_ada, b_ada, norm_mix, w_in, b_gates, q_norm, k_norm, ml_norm,
              pool_w, pool_scale, w_out, norm_ffn, w_ffn_gate, w_ffn_up, w_ffn_down, final_norm):
    ROWS = x.shape[1] // GRID_W
    cos, sin = axial_rope_tables(ROWS)
    xl, xc = x, ctx
    for layer in range(DEPTH):
        xl, xc = hybrid_layer(xl, xc, c, c_ctx, w_ada[layer], b_ada[layer], norm_mix[layer], w_in[layer],
                              b_gates[layer], q_norm[layer], k_norm[layer], ml_norm[layer], pool_w[layer],
                              pool_scale[layer], w_out[layer], norm_ffn[layer], w_ffn_gate[layer],
                              w_ffn_up[layer], w_ffn_down[layer], cos, sin, layer < DEPTH - 1)
    return rms_norm(xl, final_norm)
```

```python
import numpy as np
from contextlib import ExitStack
import concourse.bass as bass
import concourse.mybir as mybir
from concourse.bass_utils import run_bass_kernel_spmd

F32 = mybir.dt.float32
BF16 = mybir.dt.bfloat16
ALU = mybir.AluOpType
AF = mybir.ActivationFunctionType
AX = mybir.AxisListType

ENGS = ["sync", "scalar", "vector", "gpsimd", "tensor"]

D = 1024
NTOK = 2304
NT = 18
DFF = 2816
NF = 22
INW = 2064
CH = [(0, 256), (256, 512), (768, 512), (1280, 512), (1792, 512)]
EPS = 1e-6
NEG = -30000.0

C_IDENT = 0
C_NEGUF = 128
C_NEGUB = 256
C_MNF = 384
C_MNB = 512
C_NEGONES = 640
C_COS = 768
C_SIN = 1792
C_LAY = 2816
L_BADA, L_NMIX, L_NFFN, L_MLG, L_PSC, L_QG, L_KG, L_BG = 0, 48, 56, 64, 66, 68, 132, 196
LAYW = 212
C_FN = C_LAY + 2 * LAYW
C_CSEL = C_FN + 8
C_EDGE = C_CSEL + 24
CW = C_EDGE + 32


class Buf:
    __slots__ = ("name", "w", "r", "rd")

    def __init__(self, name=""):
        self.name = name
        self.w = None
        self.r = {}
        self.rd = []


class Op:
    __slots__ = ("eng", "pos", "fn", "deps", "is_dma", "dsem", "dval", "signal", "sigval", "prewait")


class Sched:
    def __init__(self, n_dma=16):
        self.ops = {e: [] for e in ENGS}
        self.seen = {e: {} for e in ENGS}
        self.n_dma = n_dma
        self.pool_base = {"sync": 0, "gpsimd": n_dma, "scalar": 2 * n_dma}
        self.dma_tot = [0] * (3 * n_dma)
        self.dma_rr = {"sync": 0, "gpsimd": 0, "scalar": 0}
        self.pending = {e: [] for e in ENGS}
        self.dma_since_bar = []

    def _add_dep(self, o, d, deps):
        if d.is_dma:
            key = ("d", d.dsem)
            val = d.dval
        else:
            key = ("e", d.eng)
            val = d.pos
        if self.seen[o.eng].get(key, 0) >= val:
            return
        self.seen[o.eng][key] = val
        deps.append(d)

    def op(self, eng, fn, reads=(), writes=(), dma=False):
        o = Op()
        o.eng = eng
        o.fn = fn
        o.is_dma = dma
        o.signal = False
        o.sigval = 0
        o.prewait = None
        o.dsem = -1
        o.dval = 0
        o.pos = len(self.ops[eng]) + 1
        cand = []
        for b in reads:
            if b.w is not None:
                cand.append((b.w, True))
        for b in writes:
            if b.w is not None:
                cand.append((b.w, False))
            for r in b.r.values():
                cand.append((r, False))
            for r in b.rd:
                cand.append((r, False))
        deps = []
        if self.pending[eng]:
            for d in self.pending[eng]:
                self._add_dep(o, d, deps)
            self.pending[eng] = []
        for d, raw in cand:
            if (not d.is_dma) and (not dma) and d.eng == eng:
                if eng == "tensor":
                    continue
                if not raw:
                    continue
            self._add_dep(o, d, deps)
        if dma:
            base = self.pool_base[eng]
            k = base + self.dma_rr[eng]
            self.dma_rr[eng] = (self.dma_rr[eng] + 1) % self.n_dma
            prev = self.dma_tot[k]
            if prev > 0 and self.seen[eng].get(("d", k), 0) < prev:
                o.prewait = (k, prev)
                self.seen[eng][("d", k)] = prev
            self.dma_tot[k] = prev + 16
            o.dsem = k
            o.dval = prev + 16
            self.dma_since_bar.append(o)
        for d in deps:
            d.signal = True
        o.deps = deps
        self.ops[eng].append(o)
        for b in reads:
            if dma:
                b.rd.append(o)
            else:
                b.r[eng] = o
        for b in writes:
            b.w = o
            b.r = {}
            b.rd = []
        return o

    def barrier(self, engs=None):
        L = []
        for e in ENGS:
            for o in reversed(self.ops[e]):
                if not o.is_dma:
                    L.append(o)
                    break
        L.extend(self.dma_since_bar)
        if engs is None:
            self.dma_since_bar = []
        for e in (ENGS if engs is None else engs):
            self.pending[e] = self.pending[e] + L

    def emit(self, nc, stack):
        sem_e = {e: stack.enter_context(nc.semaphore("se_" + e)) for e in ENGS}
        sem_d = [stack.enter_context(nc.semaphore("sd_%d" % i)) for i in range(3 * self.n_dma)]
        for e in ENGS:
            c = 0
            for o in self.ops[e]:
                if (not o.is_dma) and o.signal:
                    c += 1
                    o.sigval = c

        def run(e):
            def body(engine):
                for o in self.ops[e]:
                    for d in o.deps:
                        if d.is_dma:
                            engine.wait_ge(sem_d[d.dsem], d.dval)
                        else:
                            engine.wait_ge(sem_e[d.eng], d.sigval)
                    if o.prewait is not None:
                        engine.wait_ge(sem_d[o.prewait[0]], o.prewait[1])
                    ins = o.fn(engine)
                    if o.is_dma:
                        ins.then_inc(sem_d[o.dsem], 16)
                    elif o.signal:
                        ins.then_inc(sem_e[e], 1)
            return body

        with nc.Block() as block:
            block.sync(run("sync"))
            block.scalar(run("scalar"))
            block.vector(run("vector"))
            block.gpsimd(run("gpsimd"))
            block.tensor(run("tensor"))


ARENA_W = 53200


class _Stop(Exception):
    pass


def build_nc(n_b=2, n_layers=2, dbg=None, stop_after=None):
    nc = bass.Bass("TRN2", target_bir_lowering=False)
    xin = nc.dram_tensor("xin", [2, 8, 128, NTOK], F32, kind="ExternalInput").ap()
    cstd = nc.dram_tensor("cst", [128, CW], F32, kind="ExternalInput").ap()
    pwbd = nc.dram_tensor("pwbd", [2, 2, 128, 128], F32, kind="ExternalInput").ap()
    w_ada = nc.dram_tensor("w_ada", [2, D, 6 * D], F32, kind="ExternalInput").ap()
    w_in = nc.dram_tensor("w_in", [2, D, INW], F32, kind="ExternalInput").ap()
    w_out = nc.dram_tensor("w_out", [2, D, D], F32, kind="ExternalInput").ap()
    w_g = nc.dram_tensor("w_g", [2, D, DFF], F32, kind="ExternalInput").ap()
    w_u = nc.dram_tensor("w_u", [2, D, DFF], F32, kind="ExternalInput").ap()
    w_d = nc.dram_tensor("w_d", [2, DFF, D], F32, kind="ExternalInput").ap()
    yT = nc.dram_tensor("yT", [2, 8, 128, 2048], F32, kind="ExternalOutput").ap()
    xs = nc.dram_tensor("xs", [2, 8, 128, NTOK], F32, kind="Internal").ap()
    xs2 = nc.dram_tensor("xs2", [8, 128, NTOK], F32, kind="Internal").ap()
    dbg_t = {}
    if dbg:
        for name, shape in dbg.items():
            dbg_t[name] = nc.dram_tensor("dbg_" + name, list(shape), F32, kind="ExternalOutput").ap()

    S = Sched()
    st = ExitStack()
    arena = st.enter_context(nc.sbuf_tensor("arena", [128, ARENA_W], F32))
    ps = [st.enter_context(nc.psum_tensor("ps%d" % i, [128, 512], F32)) for i in range(8)]
    PB = [Buf("ps%d" % i) for i in range(8)]

    def R(off, n, dt=F32):
        if dt == F32:
            assert off + n <= ARENA_W, (off, n)
            return arena[:, off:off + n]
        nw = (n + 1) // 2
        assert off + nw <= ARENA_W, (off, n)
        return arena[:, off:off + nw].bitcast(BF16)[:, 0:n]

    def psb(i):
        return ps[i][:, :].bitcast(BF16)

    def mm(out, lhsT, rhs, start, stop, reads, writes, skip=False):
        if skip:
            S.op("tensor", lambda e: e.matmul(out, lhsT=lhsT, rhs=rhs, start=start, stop=stop, skip_group_check=True), reads, writes)
        else:
            S.op("tensor", lambda e: e.matmul(out, lhsT=lhsT, rhs=rhs, start=start, stop=stop), reads, writes)

    def tr(out, in_, ident, reads, writes):
        S.op("tensor", lambda e: e.transpose(out, in_, ident), reads, writes)

    def act(out, in_, func, reads, writes, bias=None, scale=None):
        kw = {}
        if bias is not None:
            kw["bias"] = bias
        if scale is not None:
            kw["scale"] = scale
        S.op("scalar", lambda e: e.activation(out=out, in_=in_, func=func, **kw), reads, writes)

    def tt(eng, out, in0, in1, op, reads, writes):
        S.op(eng, lambda e: e.tensor_tensor(out=out, in0=in0, in1=in1, op=op), reads, writes)

    def ts(eng, out, in0, s1, s2, op0, op1, reads, writes):
        if s2 is None:
            S.op(eng, lambda e: e.tensor_scalar(out=out, in0=in0, scalar1=s1, scalar2=None, op0=op0), reads, writes)
        else:
            S.op(eng, lambda e: e.tensor_scalar(out=out, in0=in0, scalar1=s1, scalar2=s2, op0=op0, op1=op1), reads, writes)

    def stt(eng, out, in0, scalar, in1, op0, op1, reads, writes):
        S.op(eng, lambda e: e.scalar_tensor_tensor(out=out, in0=in0, scalar=scalar, in1=in1, op0=op0, op1=op1), reads, writes)

    def cp(eng, out, in_, reads, writes):
        S.op(eng, lambda e: e.tensor_copy(out=out, in_=in_), reads, writes)

    def red(out, in_, reads, writes):
        S.op("vector", lambda e: e.tensor_reduce(out=out, in_=in_, axis=AX.X, op=ALU.add), reads, writes)

    def recip(out, in_, reads, writes):
        S.op("vector", lambda e: e.reciprocal(out=out, in_=in_), reads, writes)

    def mset(eng, ap, val, writes):
        S.op(eng, lambda e: e.memset(ap, val), (), writes)

    def dma(eng, out, in_, reads, writes):
        return S.op(eng, lambda e: e.dma_start(out=out, in_=in_), reads, writes, dma=True)

    out_bufs = []

    def dump(name, apf, reads):
        if dbg and name in dbg_t:
            b = Buf("dbg_" + name)
            out_bufs.append(b)
            dma("gpsimd", dbg_t[name], apf(), reads, [b])

    off = 0
    cst = R(off, CW); off += CW
    pw_sb = R(off, 2 * 2 * 128, BF16).rearrange("p (l j n) -> p l j n", l=2, j=2); off += 256
    ident_bf = R(off, 128, BF16); off += 64
    ones_bf = R(off, 128, BF16); off += 64
    mods = R(off, 2 * 3 * 48).rearrange("p (l r m) -> p l r m", l=2, r=3); off += 288
    Cst = R(off, 260).rearrange("p (j n) -> p j n", j=2); off += 260
    CTb = R(off, 260, BF16).rearrange("p (j n) -> p j n", j=2); off += 130
    scT = R(off, 24, BF16).rearrange("p (k r) -> p k r", r=3); off += 12
    cbf = R(off, 768, BF16); off += 384
    off = (off + 7) // 8 * 8
    P_END = off
    B_cst, B_pw, B_ident, B_ones, B_mods, B_C, B_CTb, B_scT = (Buf(n) for n in
                                                                ["cst", "pw", "identb", "onesb", "mods", "C", "CTb", "scT"])

    ident_f = cst[:, C_IDENT:C_IDENT + 128]
    negones_b = cbf[:, C_NEGONES:C_NEGONES + 128]
    negU = [cbf[:, C_NEGUF:C_NEGUF + 128], cbf[:, C_NEGUB:C_NEGUB + 128]]
    Mn = [cbf[:, C_MNF:C_MNF + 128], cbf[:, C_MNB:C_MNB + 128]]
    cos_t = cst[:, C_COS:C_COS + 1024].rearrange("p (t d) -> p t d", d=64)
    sin_t = cst[:, C_SIN:C_SIN + 1024].rearrange("p (t d) -> p t d", d=64)

    def lay(l, o, n):
        return cst[:, C_LAY + l * LAYW + o:C_LAY + l * LAYW + o + n]

    AB0 = P_END
    off = AB0
    kTz = R(off, 4 * NTOK, BF16).rearrange("p (j u n) -> p j u n", j=2, u=2); off += 2 * NTOK
    qT = R(off, 4 * NTOK, BF16).rearrange("p (j n) -> p j n", j=4); off += 2 * NTOK
    mqT = R(off, 2 * NTOK, BF16).rearrange("p (j n) -> p j n", j=2); off += NTOK
    mkT = R(off, 2 * NTOK, BF16).rearrange("p (j n) -> p j n", j=2); off += NTOK
    Vaug = R(off, NT * 130, BF16).rearrange("p (t h c) -> p t h c", t=NT, h=2); off += NT * 65
    mk_tok = R(off, NT * 256, BF16).rearrange("p (t c) -> p t c", t=NT); off += NT * 128
    mVaug = R(off, NT * 260, BF16).rearrange("p (t h c) -> p t h c", t=NT, h=4); off += NT * 130
    moT = R(off, 2 * NTOK, BF16).rearrange("p (j n) -> p j n", j=2); off += NTOK
    G = R(off, NT * 16).rearrange("p (t c) -> p t c", t=NT); off += NT * 16
    UW = 2352
    uT = R(off, 2 * UW, BF16).rearrange("p (j n) -> p j n", j=2); off += UW
    off = (off + 7) // 8 * 8
    OV0 = off

    def mixT(kc):
        if kc < 4:
            return qT[:, kc, :]
        if kc < 6:
            return mqT[:, kc - 4, :]
        return mkT[:, kc - 6, :]

    B_qT = [Buf("qT%d" % c) for c in range(5)]
    B_kTd = [Buf("kTd%d" % t) for t in range(NT)]
    B_V = [Buf("V%d" % t) for t in range(NT)]
    B_mqT = [Buf("mqT%d" % t) for t in range(NT)]
    B_mkT = [Buf("mkT%d" % t) for t in range(NT)]
    B_mk = [Buf("mk%d" % t) for t in range(NT)]
    B_mV = [Buf("mV%d" % t) for t in range(NT)]
    B_moT = [Buf("moT%d" % c) for c in range(5)]
    B_G = [Buf("G%d" % t) for t in range(NT)]
    B_uT = [Buf("uT%d" % c) for c in range(5)]
    B_uTall = Buf("uTall")
    B_xs = [[Buf("xs%d_%d" % (b, c)) for c in range(5)] for b in range(2)]
    B_xs2 = [Buf("xs2_%d" % c) for c in range(5)]

    def tiles_of(c):
        o, n = CH[c]
        return list(range(o // 128, (o + n) // 128))

    dma("sync", cst, cstd, [], [B_cst])
    for l in range(2):
        for j in range(2):
            dma("gpsimd", pw_sb[:, l, j, :], pwbd[l, j], [], [B_pw])
    cp("vector", ident_bf, ident_f, [B_cst], [B_ident])
    cp("vector", cbf, cst[:, 0:768], [B_cst], [B_cst])
    mset("vector", ones_bf, 1.0, [B_ones])
    csel = cst[:, C_CSEL:C_CSEL + 24].rearrange("p (k r) -> p k r", r=3)
    act(scT, csel, AF.Silu, [B_cst], [B_scT])
    def ada_dma(lw, piece, wa, B_wa):
        for kc in range(8):
            dma("gpsimd", wa[:, kc, :], w_ada[lw].rearrange("(k p) n -> p k n", p=128)[:, kc, piece * 1024:(piece + 1) * 1024],
                [], [B_wa])

    def ada_mm(lw, piece, wa, B_wa, bank, col0):
        for m8 in range(8):
            for kc in range(8):
                mm(ps[bank][:, col0 + m8 * 3:col0 + (m8 + 1) * 3], wa[:, kc, m8 * 128:(m8 + 1) * 128], scT[:, kc, :],
                   kc == 0, kc == 7, [B_wa, B_scT], [PB[bank]])
        tt("vector", mods[:, lw, :, piece * 8:(piece + 1) * 8].rearrange("p r m -> p m r"),
           ps[bank][:, col0:col0 + 24].rearrange("p (m r) -> p m r", r=3),
           lay(lw, L_BADA + piece * 8, 8).unsqueeze(2).to_broadcast([128, 8, 3]), ALU.add, [PB[bank], B_cst], [B_mods])
        if piece == 1:
            stt("vector", mods[:, lw, :, 8:16], mods[:, lw, :, 8:16], 1.0,
                lay(lw, L_NMIX, 8).unsqueeze(1).to_broadcast([128, 3, 8]), ALU.add, ALU.mult, [B_mods, B_cst], [B_mods])
        if piece == 4:
            stt("vector", mods[:, lw, :, 32:40], mods[:, lw, :, 32:40], 1.0,
                lay(lw, L_NFFN, 8).unsqueeze(1).to_broadcast([128, 3, 8]), ALU.add, ALU.mult, [B_mods, B_cst], [B_mods])

    wa_sb = R(OV0, 8 * 1024, BF16).rearrange("p (k n) -> p k n", k=8)
    B_wa = Buf("wa")
    for piece in range(2):
        ada_dma(0, piece, wa_sb, B_wa)
        ada_mm(0, piece, wa_sb, B_wa, 0, piece * 24)
    ada_todo = [(0, p_) for p_ in range(2, 6)] + [(l_, p_) for l_ in range(1, n_layers) for p_ in range(6)]
    S.barrier()

    def stop(tag):
        if stop_after == tag:
            raise _Stop()

    def modcol(l, r, m):
        return mods[:, l, r, m:m + 1]

    def rms_rstd(xc, n, sqr, lnr, rstd, B_xc, B_sqr, B_lnr, B_rstd, bank):
        for kc in range(8):
            act(sqr[kc % 2][:, 0:n], xc[:, kc, 0:n], AF.Square, [B_xc], [B_sqr[kc % 2]])
            mm(ps[bank][:, 0:n], ones_bf, sqr[kc % 2][:, 0:n], kc == 0, kc == 7, [B_ones, B_sqr[kc % 2]], [PB[bank]])
        act(lnr[:, 0:n], ps[bank][:, 0:n], AF.Ln, [PB[bank]], [B_lnr], bias=EPS, scale=1.0 / D)
        act(rstd[:, 0:n], lnr[:, 0:n], AF.Exp, [B_lnr], [B_rstd], scale=-0.5)

    WS_OFF = ARENA_W - 8 * NTOK // 2
    win = R(WS_OFF, 8 * INW, BF16).rearrange("p (k n) -> p k n", k=8)
    B_win = Buf("win")
    win_state = {"loaded_for": None}

    def load_win(l_, extra_w=()):
        for kc in range(8):
            dma("gpsimd", win[:, kc, :], w_in[l_].rearrange("(k p) n -> p k n", p=128)[:, kc, :], [], [B_win] + list(extra_w))

    fin_todo = []
    try:
        stop('setup')
        for bi in range(n_b):
            for l in range(n_layers):
                src = xin if l == 0 else xs
                B_src = [Buf("xin%d" % c) for c in range(5)] if l == 0 else B_xs[bi]
                chunks = [0, 1, 2, 3, 4] if l == 0 else [1, 2, 3, 4]

                o = OV0
                xc1 = R(o, 8 * 512).rearrange("p (k n) -> p k n", k=8); o += 4096
                xcA = [xc1, xc1]
                hTA = [R(o + i * 2048, 8 * 512, BF16).rearrange("p (k n) -> p k n", k=8) for i in range(2)]; o += 4096
                sqr = [R(o + i * 256, 512, BF16) for i in range(2)]; o += 512
                rstd = R(o, 512); o += 512
                tmpAA = [R(o + i * 512, 512) for i in range(2)]; o += 1024
                lnr = tmpAA[1]
                qk32 = R(o, 640); o += 640
                t1 = R(o, 640); o += 640
                sq = t1
                t2 = R(o, 640); o += 640
                q_bfs = [R(o + i * 256, 512, BF16) for i in range(2)]; o += 512
                kd_bfs = [R(o + i * 128, 256, BF16) for i in range(2)]; o += 256
                ssq = R(o, 16); o += 16
                lnq = R(o, 16); o += 16
                rq = R(o, 16); o += 16
                assert o <= WS_OFF, (o, WS_OFF)
                B_xc1 = Buf("xc")
                B_xcA = [B_xc1, B_xc1]
                B_hTA = [Buf("hT0"), Buf("hT1")]
                B_sqr = [Buf("sqr0"), Buf("sqr1")]
                B_tmpAA = [Buf("tmpA0"), Buf("tmpA1")]
                B_lnr, B_rstd = Buf("lnr"), Buf("rstd")
                B_qk32, B_t1, B_t2, B_ssq, B_lnq, B_rq = (Buf(n) for n in ["qk32", "t1", "t2", "ssq", "lnq", "rq"])
                B_qbfs = [Buf("qbf0"), Buf("qbf1")]
                B_kdbfs = [Buf("kdbf0"), Buf("kdbf1")]
                late_q = []
                tile_ctr = [0]
                B_sq = B_t1
                if win_state["loaded_for"] != (bi, l):
                    load_win(l)
                    win_state["loaded_for"] = (bi, l)
                mset("gpsimd", Vaug.rearrange("p t h c -> p (t h c)"), 1.0, B_V)
                mset("gpsimd", mVaug.rearrange("p t h c -> p (t h c)"), 1.0, B_mV)
                mset("gpsimd", uT.rearrange("p j n -> p (j n)"), 0.0, B_uT + [B_uTall])
                mset("gpsimd", kTz.rearrange("p j u n -> p (j u n)"), 0.0, B_kTd)

                def prepA(c):
                    off_c, n = CH[c]
                    r = 2 if c == 0 else bi
                    xc, hT, B_xc, B_hT = xcA[c % 2], hTA[c % 2], B_xcA[c % 2], B_hTA[c % 2]
                    dma("sync", xc[:, :, 0:n], src[bi].rearrange("k p n -> p k n")[:, :, off_c:off_c + n], [B_src[c]], [B_xc])
                    rms_rstd(xc, n, sqr, lnr, rstd, B_xc, B_sqr, B_lnr, B_rstd, 0)
                    for kc in range(8):
                        tA, B_tA = tmpAA[kc % 2], B_tmpAA[kc % 2]
                        stt("vector", tA[:, 0:n], xc[:, kc, 0:n], modcol(l, r, 8 + kc), rstd[:, 0:n], ALU.mult, ALU.mult,
                            [B_xc, B_mods, B_rstd], [B_tA])
                        act(hT[:, kc, 0:n], tA[:, 0:n], AF.Identity, [B_tA, B_mods], [B_hT], bias=modcol(l, r, kc))

                prepA(0)
                tok_set = 0
                fm_rot = 0
                for c in range(5):
                    off_c, n = CH[c]
                    r = 2 if c == 0 else bi
                    if c + 1 < 5:
                        prepA(c + 1)
                    hT, B_hT = hTA[c % 2], B_hTA[c % 2]
                    for lt in range(n // 128):
                        tt_i = off_c // 128 + lt
                        bA, bB, bC = (1, 2, 3) if tok_set == 0 else (4, 5, 6)
                        tok_set ^= 1
                        hsl = slice(lt * 128, (lt + 1) * 128)
                        q_bf, kd_bf = q_bfs[tile_ctr[0] % 2], kd_bfs[tile_ctr[0] % 2]
                        B_qbf, B_kdbf = B_qbfs[tile_ctr[0] % 2], B_kdbfs[tile_ctr[0] % 2]
                        tile_ctr[0] += 1
                        for kc in range(8):
                            mm(ps[bA][:, 0:512], hT[:, kc, hsl], win[:, kc, 0:512], kc == 0, kc == 7, [B_hT, B_win], [PB[bA]])
                        for kc in range(8):
                            mm(ps[bB][:, 0:256], hT[:, kc, hsl], win[:, kc, 512:768], kc == 0, kc == 7, [B_hT, B_win], [PB[bB]])
                        for kc in range(8):
                            mm(ps[bB][:, 256:272], hT[:, kc, hsl], win[:, kc, 1792:1808], kc == 0, kc == 7, [B_hT, B_win], [PB[bB]])
                        for kc in range(8):
                            mm(ps[bC][:, 0:512], hT[:, kc, hsl], win[:, kc, 1024:1536], kc == 0, kc == 7, [B_hT, B_win], [PB[bC]])
                        if len(late_q) >= 2:
                            late_q.pop(0)()
                        act(qk32[:, 0:512], ps[bA][:, 0:512], AF.Copy, [PB[bA]], [B_qk32])
                        act(qk32[:, 512:640], ps[bB][:, 0:128], AF.Copy, [PB[bB]], [B_qk32])
                        tt("vector", sq, qk32, qk32, ALU.mult, [B_qk32], [B_sq])
                        red(ssq[:, 0:10], sq.rearrange("p (h d) -> p h d", d=64), [B_sq], [B_ssq])
                        act(lnq[:, 0:10], ssq[:, 0:10], AF.Ln, [B_ssq], [B_lnq], bias=EPS, scale=1.0 / 64)
                        act(rq[:, 0:10], lnq[:, 0:10], AF.Exp, [B_lnq], [B_rq], scale=-0.5)
                        qk3 = qk32.rearrange("p (h d) -> p h d", d=64)
                        tt("vector", qk3, qk3, rq[:, 0:10].unsqueeze(2).to_broadcast([128, 10, 64]), ALU.mult, [B_qk32, B_rq], [B_qk32])
                        tt("vector", qk3[:, 0:8, :], qk3[:, 0:8, :], lay(l, L_QG, 64).unsqueeze(1).to_broadcast([128, 8, 64]),
                           ALU.mult, [B_qk32, B_cst], [B_qk32])
                        tt("vector", qk3[:, 8:10, :], qk3[:, 8:10, :], lay(l, L_KG, 64).unsqueeze(1).to_broadcast([128, 2, 64]),
                           ALU.mult, [B_qk32, B_cst], [B_qk32])
                        kd4 = kd_bf.rearrange("p (h u d) -> p h u d", h=2, u=2)
                        if c >= 1:
                            lat_tile = tt_i - 2
                            t13 = t1.rearrange("p (h d) -> p h d", d=64)
                            tt("vector", t13, qk3, cos_t[:, lat_tile, :].unsqueeze(1).to_broadcast([128, 10, 64]), ALU.mult,
                               [B_qk32, B_cst], [B_t1])
                            q5 = qk32.rearrange("p (h a f q) -> p h a f q", h=10, a=2, f=2)
                            t25 = t2.rearrange("p (h a f q) -> p h a f q", h=10, a=2, f=2)
                            s4 = sin_t[:, lat_tile, :].rearrange("p (a f q) -> p a f q", a=2, f=2)
                            for f in range(2):
                                tt("vector", t25[:, :, :, f, :], q5[:, :, :, 1 - f, :],
                                   s4[:, :, f, :].unsqueeze(1).to_broadcast([128, 10, 2, 16]), ALU.mult, [B_qk32, B_cst], [B_t2])
                            tt("vector", q_bf, t1[:, 0:512], t2[:, 0:512], ALU.add, [B_t1, B_t2], [B_qbf])
                            t1k = t1[:, 512:640].rearrange("p (h d) -> p h d", d=64).unsqueeze(2).to_broadcast([128, 2, 2, 64])
                            t2k = t2[:, 512:640].rearrange("p (h d) -> p h d", d=64).unsqueeze(2).to_broadcast([128, 2, 2, 64])
                            tt("vector", kd4, t1k, t2k, ALU.add, [B_t1, B_t2], [B_kdbf])
                        else:
                            cp("vector", q_bf, qk32[:, 0:512], [B_qk32], [B_qbf])
                            cp("vector", kd4, qk32[:, 512:640].rearrange("p (h d) -> p h d", d=64).unsqueeze(2).to_broadcast([128, 2, 2, 64]),
                               [B_qk32], [B_kdbf])
                        if bi == 0 and l == 0 and tt_i == 2:
                            dump("qk32", lambda: qk32, [B_qk32])
                        def late(q_bf=q_bf, kd_bf=kd_bf, B_qbf=B_qbf, B_kdbf=B_kdbf, tt_i=tt_i, c=c):
                            pT = psb(7)
                            for j in range(4):
                                tr(pT[:, j * 128:(j + 1) * 128], q_bf[:, j * 128:(j + 1) * 128], ident_bf, [B_qbf, B_ident], [PB[7]])
                            for kv in range(2):
                                tr(pT[:, 512 + kv * 128:512 + (kv + 1) * 128], kd_bf[:, kv * 128:(kv + 1) * 128], ident_bf,
                                   [B_kdbf, B_ident], [PB[7]])
                            tsl = slice(tt_i * 128, (tt_i + 1) * 128)
                            act(qT[:, :, tsl], pT[:, 0:512].rearrange("p (j n) -> p j n", j=4), AF.Copy, [PB[7]], [B_qT[c]])
                            for u_ in range(2):
                                act(kTz[u_ * 64:(u_ + 1) * 64, :, u_, tsl], pT[u_ * 64:(u_ + 1) * 64, 512:768].rearrange("p (j n) -> p j n", j=2),
                                    AF.Copy, [PB[7]], [B_kTd[tt_i]])
                        late_q.append(late)
                        act(Vaug[:, tt_i, :, 0:64], ps[bB][:, 128:256].rearrange("p (h d) -> p h d", d=64), AF.Copy, [PB[bB]], [B_V[tt_i]])
                        act(mk_tok[:, tt_i, :], ps[bC][:, 0:256], AF.Copy, [PB[bC]], [B_mk[tt_i]], scale=0.125)
                        cp("vector", mVaug[:, tt_i, :, 0:64], ps[bC][:, 256:512].rearrange("p (h d) -> p h d", d=64), [PB[bC]], [B_mV[tt_i]])
                        tt("vector", G[:, tt_i, :], ps[bB][:, 256:272], lay(l, L_BG, 16), ALU.add, [PB[bB], B_cst], [B_G[tt_i]])
                        gf = G[:, tt_i, :].rearrange("p (d g) -> p d g", d=2)[:, :, 4:8]
                        act(gf, gf, AF.Exp, [B_G[tt_i]], [B_G[tt_i]], scale=-1.0)
                        act(gf, gf, AF.Ln, [B_G[tt_i]], [B_G[tt_i]], bias=1.0)
                    til = tiles_of(c)
                    upos = 8 if c == 0 else off_c + 24
                    fm = [(768, "mq", 0), (896, "mq", 1), (1024, "mk", 0), (1152, "mk", 1),
                          (1536, "mo", 0), (1664, "mo", 1), (1808, "pp", 0), (1936, "pp", 1)]
                    for (c0, kind, j) in fm:
                        bank = 1 + fm_rot % 6
                        fm_rot += 1
                        if c == 4 and late_q:
                            late_q.pop(0)()
                        for kc in range(8):
                            mm(ps[bank][:, 0:n], win[:, kc, c0:c0 + 128], hT[:, kc, 0:n], kc == 0, kc == 7, [B_win, B_hT], [PB[bank]])
                        if kind == "mq":
                            act(mqT[:, j, off_c:off_c + n], ps[bank][:, 0:n], AF.Copy, [PB[bank]], [B_mqT[t] for t in til])
                        elif kind == "mk":
                            act(mkT[:, j, off_c:off_c + n], ps[bank][:, 0:n], AF.Copy, [PB[bank]], [B_mkT[t] for t in til], scale=0.125)
                        elif kind == "mo":
                            act(moT[:, j, off_c:off_c + n], ps[bank][:, 0:n], AF.Sigmoid, [PB[bank]], [B_moT[c]])
                        else:
                            cp("vector", uT[:, j, upos:upos + n], ps[bank][:, 0:n], [PB[bank]], [B_uT[c]])
                while late_q:
                    late_q.pop(0)()
                if bi == 0 and l == 0:
                    dump("qT", lambda: qT[:, :, 256:768], B_qT)
                    dump("kTd", lambda: kTz[:, :, 0, 256:768], B_kTd)
                    dump("G", lambda: G.rearrange("p t c -> p (t c)"), B_G)
                S.barrier()
                stop('A')

                o = OV0
                Hsum = R(o, NT * 256).rearrange("p (t h d) -> p t h d", t=NT, h=4); o += NT * 256
                PT = [R(o, 512, BF16), R(o + 256, 512, BF16)]; o += 512
                att_tok = R(o, 4 * 512, BF16).rearrange("p (q f) -> p q f", q=4); o += 1024
                rinv = R(o, 8); o += 8
                CSs = R(o, NT * 16).rearrange("p (t c) -> p t c", t=NT); o += NT * 16
                biasD = R(o, NT * 8).rearrange("p (t c) -> p t c", t=NT); o += NT * 8
                wkarg = R(o, NT * 8).rearrange("p (t c) -> p t c", t=NT); o += NT * 8
                wk = R(o, NT * 8).rearrange("p (t c) -> p t c", t=NT); o += NT * 8
                dstate = R(o, NT * 8).rearrange("p (t c) -> p t c", t=NT); o += NT * 8
                eb = R(o, NT * 8).rearrange("p (t c) -> p t c", t=NT); o += NT * 8
                LFhi = [R(o + i * 256, 512, BF16).rearrange("p (h n) -> p h n", h=4) for i in range(2)]; o += 512
                LFlo = [R(o + i * 256, 512, BF16).rearrange("p (h n) -> p h n", h=4) for i in range(2)]; o += 512
                Ghi = R(o, NT * 16, BF16).rearrange("p (t c) -> p t c", t=NT); o += NT * 8
                Glo = R(o, NT * 16, BF16).rearrange("p (t c) -> p t c", t=NT); o += NT * 8
                B_Ghl = Buf("Ghl")
                DT = [R(o, 512).rearrange("p (h n) -> p h n", h=4), R(o + 512, 512).rearrange("p (h n) -> p h n", h=4)]; o += 1024
                PTm = [R(o, 512, BF16).rearrange("p (h n) -> p h n", h=4), R(o + 256, 512, BF16).rearrange("p (h n) -> p h n", h=4)]; o += 512
                tmpO = R(o, 260).rearrange("p (h c) -> p h c", h=4); o += 260
                Ocomb = R(o, 260).rearrange("p (h c) -> p h c", h=4); o += 260
                dd = R(o, 4); o += 4
                rr = R(o, 4); o += 4
                hdir = R(o, 256).rearrange("p (h d) -> p h d", h=4); o += 256
                mkw = R(o, 256, BF16); o += 128
                sqh = R(o, 256).rearrange("p (h d) -> p h d", h=4); o += 256
                ssh = R(o, 4); o += 4
                lnh = R(o, 4); o += 4
                rh = R(o, 4); o += 4
                hn = R(o, 256, BF16); o += 128
                PW = 544
                z32 = R(o, 2 * PW).rearrange("p (j n) -> p j n", j=2); o += 2 * PW
                pX = R(o, 2 * PW).rearrange("p (j n) -> p j n", j=2); o += 2 * PW
                pY = R(o, 2 * PW).rearrange("p (j n) -> p j n", j=2); o += 2 * PW
                ppT = R(o, 2 * 512, BF16).rearrange("p (j n) -> p j n", j=2); o += 512
                etmp = R(o, 8); o += 8
                assert o <= ARENA_W
                B_H = [Buf("H%d" % t) for t in range(NT)]
                B_PT = [Buf("PT0"), Buf("PT1")]
                B_att, B_rinv, B_CSs, B_bD, B_wka, B_wk, B_ds, B_eb = (Buf(n) for n in ["att", "rinv", "CSs", "bD", "wka", "wk", "ds", "eb"])
                B_LF = [Buf("LF0"), Buf("LF1")]
                B_DT = [Buf("DT0"), Buf("DT1")]
                B_PTm = [Buf("PTm0"), Buf("PTm1")]
                B_tmpO, B_Oc, B_dd, B_rr, B_hdir, B_mkw, B_sqh, B_ssh, B_lnh, B_rh, B_hn = (Buf(n) for n in [
                    "tmpO", "Oc", "dd", "rr", "hdir", "mkw", "sqh", "ssh", "lnh", "rh", "hn"])
                B_z, B_pX, B_pY, B_ppT, B_et = (Buf(n) for n in ["z", "pX", "pY", "ppT", "et"])

                CSp = ps[5][:, 0:NT * 16].rearrange("p (t c) -> p t c", t=NT)
                cp("vector", Ghi, G, B_G, [B_Ghl])
                tt("vector", Glo, G, Ghi, ALU.subtract, B_G + [B_Ghl], [B_Ghl])
                for t_i in range(NT):
                    for (c0_, lw, g0_) in [(0, negU[0], 4), (4, negU[1], 12), (8, negones_b, 4), (12, negones_b, 12)]:
                        mm(CSp[:, t_i, c0_:c0_ + 4], lw, Ghi[:, t_i, g0_:g0_ + 4], True, False, [B_cst, B_Ghl], [PB[5]])
                        mm(CSp[:, t_i, c0_:c0_ + 4], lw, Glo[:, t_i, g0_:g0_ + 4], False, True, [B_cst, B_Ghl], [PB[5]])
                cp("vector", CSs, CSp, [PB[5]], [B_CSs])
                for d_ in range(2):
                    tt("vector", biasD[:, :, d_ * 4:d_ * 4 + 4], G[:, :, d_ * 8:d_ * 8 + 4], CSs[:, :, d_ * 4:d_ * 4 + 4], ALU.subtract,
                       B_G + [B_CSs], [B_bD])
                tt("vector", wkarg, biasD, CSs[:, :, 8:16], ALU.add, [B_bD, B_CSs], [B_wka])
                act(wk, wkarg, AF.Exp, [B_wka], [B_wk])
                act(dstate, CSs[:, :, 8:16], AF.Exp, [B_CSs], [B_ds])
                act(eb, CSs[:, :, 0:8], AF.Exp, [B_CSs], [B_eb])

                def ml_stage0(d, t_i, pp, want_out):
                    cp("vector", LFhi[pp], Ghi[:, t_i, d * 8 + 4:d * 8 + 8].unsqueeze(2).to_broadcast([128, 4, 128]), [B_Ghl], [B_LF[pp]])
                    cp("vector", LFlo[pp], Glo[:, t_i, d * 8 + 4:d * 8 + 8].unsqueeze(2).to_broadcast([128, 4, 128]), [B_Ghl], [B_LF[pp]])

                def ml_stage1(d, t_i, pp, want_out):
                    if not want_out:
                        return
                    tsl = slice(t_i * 128, (t_i + 1) * 128)
                    Bm = ps[5][:, :].rearrange("p (h n) -> p h n", h=4)
                    for h in range(4):
                        mm(Bm[:, h, :], ident_bf, Mn[d], True, False, [B_cst, B_ident], [PB[5]])
                        mm(Bm[:, h, :], LFhi[pp][:, h, :], negU[d], False, False, [B_LF[pp], B_cst], [PB[5]])
                        mm(Bm[:, h, :], LFlo[pp][:, h, :], negU[d], False, True, [B_LF[pp], B_cst], [PB[5]])
                    for h in range(4):
                        act(DT[pp][:, h, :], Bm[:, h, :], AF.Exp, [PB[5], B_bD], [B_DT[pp]], bias=biasD[:, t_i, d * 4 + h:d * 4 + h + 1])
                    STb = [ps[6][:, 0:256].rearrange("p (j n) -> p j n", j=2), ps[7][:, 0:256].rearrange("p (j n) -> p j n", j=2)]
                    SBk = [PB[6], PB[7]]
                    for blk in range(2):
                        base = blk * 64
                        for j in range(2):
                            mm(STb[blk][:, j, :], mkT[base:base + 64, j, tsl], mqT[base:base + 64, j, tsl], True, True,
                               [B_mkT[t_i], B_mqT[t_i]], [SBk[blk]])
                    for blk in range(2):
                        tt("vector", PTm[pp].rearrange("p (j b) n -> p j b n", b=2)[:, :, blk, :], STb[blk],
                           DT[pp].rearrange("p (j b) n -> p j b n", b=2)[:, :, blk, :], ALU.mult, [SBk[blk], B_DT[pp]], [B_PTm[pp]])

                def ml_stage2(d, t_i, pp, want_out):
                    tsl = slice(t_i * 128, (t_i + 1) * 128)
                    if want_out:
                        OAb = [ps[6][:, 256:386].rearrange("p (h c) -> p h c", h=2), ps[7][:, 256:386].rearrange("p (h c) -> p h c", h=2)]
                        OB = ps[4][:, 0:260].rearrange("p (h c) -> p h c", h=4)
                        for h in range(4):
                            mm(OAb[h // 2][:, h % 2, :], PTm[pp][:, h, :], mVaug[:, t_i, h, :], True, True, [B_PTm[pp], B_mV[t_i]], [PB[6 + h // 2]])
                        for j in range(2):
                            mm(ps[4][:, j * 130:(j + 1) * 130], mqT[:, j, tsl], CTb[:, j, :], True, True, [B_mqT[t_i], B_CTb], [PB[4]])
                        tt("vector", tmpO, OB, eb[:, t_i, d * 4:d * 4 + 4].unsqueeze(2).to_broadcast([128, 4, 65]), ALU.mult,
                           [PB[4], B_eb], [B_tmpO])
                        for hp in range(2):
                            tt("vector", Ocomb[:, 2 * hp:2 * hp + 2, :], tmpO[:, 2 * hp:2 * hp + 2, :], OAb[hp], ALU.add,
                               [B_tmpO, PB[6 + hp]], [B_Oc])
                        stt("vector", dd, Ocomb[:, :, 64], -1.0, Ocomb[:, :, 64], ALU.mult, ALU.max, [B_Oc], [B_dd])
                        ts("vector", dd, dd, 1.0, None, ALU.max, None, [B_dd], [B_dd])
                        recip(rr, dd, [B_dd], [B_rr])
                        if d == 0:
                            tt("vector", Hsum[:, t_i], Ocomb[:, :, 0:64], rr.unsqueeze(2).to_broadcast([128, 4, 64]), ALU.mult,
                               [B_Oc, B_rr], [B_H[t_i]])
                        else:
                            tt("vector", hdir, Ocomb[:, :, 0:64], rr.unsqueeze(2).to_broadcast([128, 4, 64]), ALU.mult,
                               [B_Oc, B_rr], [B_hdir])
                            tt("vector", Hsum[:, t_i], Hsum[:, t_i], hdir, ALU.add, [B_H[t_i], B_hdir], [B_H[t_i]])
                    tt("vector", mkw.rearrange("p (h d) -> p h d", h=4), mk_tok[:, t_i, :].rearrange("p (h d) -> p h d", h=4),
                       wk[:, t_i, d * 4:d * 4 + 4].unsqueeze(2).to_broadcast([128, 4, 64]), ALU.mult, [B_mk[t_i], B_wk], [B_mkw])

                def ml_stage3(d, t_i, pp, want_out):
                    for j in range(2):
                        mm(ps[3][:, j * 130:(j + 1) * 130], mkw[:, j * 128:(j + 1) * 128],
                           mVaug[:, t_i, 2 * j:2 * j + 2, :].rearrange("p h c -> p (h c)"), True, True, [B_mkw, B_mV[t_i]], [PB[3]])
                    for j in range(2):
                        for blk in range(2):
                            rows = slice(blk * 64, (blk + 1) * 64)
                            cols = slice(blk * 65, (blk + 1) * 65)
                            hh = 2 * j + blk
                            stt("vector", Cst[rows, j, cols], Cst[rows, j, cols], dstate[rows, t_i, d * 4 + hh:d * 4 + hh + 1],
                                ps[3][rows, j * 130 + blk * 65:j * 130 + (blk + 1) * 65], ALU.mult, ALU.add,
                                [B_C, B_ds, PB[3]], [B_C])
                            cp("vector", CTb[rows, j, cols], Cst[rows, j, cols], [B_C], [B_CTb])

                def ml_reset(d):
                    mset("vector", Cst.rearrange("p j n -> p (j n)"), 0.0, [B_C])
                    mset("vector", CTb.rearrange("p j n -> p (j n)"), 0.0, [B_CTb])

                def ml_readout(t_i):
                    tsl = slice(t_i * 128, (t_i + 1) * 128)
                    cch = 0 if t_i < 2 else 1 + (t_i - 2) // 4
                    tt("vector", sqh, Hsum[:, t_i], Hsum[:, t_i], ALU.mult, [B_H[t_i]], [B_sqh])
                    red(ssh, sqh, [B_sqh], [B_ssh])
                    act(lnh, ssh, AF.Ln, [B_ssh], [B_lnh], bias=EPS, scale=1.0 / 64)
                    act(rh, lnh, AF.Exp, [B_lnh], [B_rh], scale=-0.5)
                    tt("vector", hn.rearrange("p (h d) -> p h d", h=4), Hsum[:, t_i], rh.unsqueeze(2).to_broadcast([128, 4, 64]), ALU.mult,
                       [B_H[t_i], B_rh], [B_hn])

                def ml_readout2(t_i):
                    tsl = slice(t_i * 128, (t_i + 1) * 128)
                    cch = 0 if t_i < 2 else 1 + (t_i - 2) // 4
                    pT = psb(3)
                    for j in range(2):
                        tr(pT[:, j * 128:(j + 1) * 128], hn[:, j * 128:(j + 1) * 128], ident_bf, [B_hn, B_ident], [PB[3]])
                    for j in range(2):
                        stt("vector", mqT[:, j, tsl], pT[:, j * 128:(j + 1) * 128], lay(l, L_MLG, 2)[:, j:j + 1], moT[:, j, tsl],
                            ALU.mult, ALU.mult, [PB[3], B_cst, B_moT[cch]], [B_mqT[t_i]])

                steps_l = []
                step = 0
                for d in range(2):
                    order = list(range(NT)) if d == 0 else [1, 0] + list(range(NT - 1, 1, -1))
                    for t_i in order:
                        want_out = (l == 0) or (t_i >= 2)
                        steps_l.append((d, t_i, step % 2, want_out))
                        step += 1
                stages = []
                ro_pending = []
                for k_, args in enumerate(steps_l):
                    d, t_i, pp_, want_out = args
                    if k_ == 0 or steps_l[k_ - 1][0] != d:
                        stages.append((ml_reset, (d,)))
                    if k_ == 0 and want_out:
                        stages.append((ml_stage0, args))
                    if k_ + 1 < len(steps_l) and steps_l[k_ + 1][3]:
                        stages.append((ml_stage0, steps_l[k_ + 1]))
                    if want_out:
                        stages.append((ml_stage1, args))
                    stages.append((ml_stage2, args))
                    if ro_pending:
                        stages.append((ml_readout2, (ro_pending.pop(0),)))
                    stages.append((ml_stage3, args))
                    if d == 1 and want_out:
                        stages.append((ml_readout, (t_i,)))
                        ro_pending.append(t_i)
                while ro_pending:
                    stages.append((ml_readout2, (ro_pending.pop(0),)))
                if bi == 0 and l == 0 and ada_todo:
                    wa2 = [R(WS_OFF + i * 4096, 8 * 1024, BF16).rearrange("p (k n) -> p k n", k=8) for i in range(2)]
                    B_wa2 = [Buf("wa2_0"), Buf("wa2_1")]

                    todo_ = list(ada_todo)

                    def ada_stage_dma(idx):
                        lw, piece = todo_[idx]
                        ada_dma(lw, piece, wa2[idx % 2], B_wa2[idx % 2])

                    def ada_stage_mm(idx):
                        lw, piece = todo_[idx]
                        ada_mm(lw, piece, wa2[idx % 2], B_wa2[idx % 2], 4, 300)

                    ins = []
                    sp_ = max(10, (len(stages) - 8) // (len(ada_todo) + 1))
                    for idx in range(len(ada_todo)):
                        ins.append(((idx + 1) if idx < 2 else sp_ * (idx - 1) + 5, (ada_stage_dma, (idx,))))
                        ins.append((sp_ * (idx + 1) + 4, (ada_stage_mm, (idx,))))
                    ins.sort(key=lambda x: x[0])
                    merged = []
                    ii = 0
                    for pos_, st_ in enumerate(stages):
                        while ii < len(ins) and ins[ii][0] <= pos_:
                            merged.append(ins[ii][1])
                            ii += 1
                        merged.append(st_)
                    while ii < len(ins):
                        merged.append(ins[ii][1])
                        ii += 1
                    stages = merged
                    ada_todo = []
                if l == 0 and fin_todo:
                    bi_f = fin_todo.pop(0)
                    fo = WS_OFF
                    xcq = R(fo, 8 * 512).rearrange("p (k n) -> p k n", k=8); fo += 4096
                    sqq = [R(fo + i * 256, 512, BF16) for i in range(2)]; fo += 512
                    lnq_ = R(fo, 512); fo += 512
                    rsq_ = R(fo, 512); fo += 512
                    B_xcq, B_lnq_, B_rsq_ = Buf("xcq"), Buf("lnq_"), Buf("rsq_")
                    B_sqq = [Buf("sqq0"), Buf("sqq1")]

                    def fin_stage2(c, bi_f=bi_f):
                        off_c, n = CH[c]
                        sqf = R(WS_OFF + 5632, 8 * 512, BF16).rearrange("p (k n) -> p k n", k=8)
                        B_sqf = B_sqq[0]
                        dma("sync", xcq[:, :, 0:n], xs[bi_f].rearrange("k p n -> p k n")[:, :, off_c:off_c + n], [B_xs[bi_f][c]], [B_xcq])
                        for kc in range(8):
                            tt("gpsimd", sqf[:, kc, 0:n], xcq[:, kc, 0:n], xcq[:, kc, 0:n], ALU.mult, [B_xcq], [B_sqf])
                        for q_ in range(4):
                            for kc in range(8):
                                mm(ps[4][:, 300:428], ones_bf, sqf[:, kc, q_ * 128:(q_ + 1) * 128], kc == 0, kc == 7,
                                   [B_ones, B_sqf], [PB[4]])
                            act(lnq_[:, q_ * 128:(q_ + 1) * 128], ps[4][:, 300:428], AF.Ln, [PB[4]], [B_lnq_], bias=EPS, scale=1.0 / D)
                        act(rsq_[:, 0:n], lnq_[:, 0:n], AF.Exp, [B_lnq_], [B_rsq_], scale=-0.5)
                        for kc in range(8):
                            stt("vector", xcq[:, kc, 0:n], xcq[:, kc, 0:n], cst[:, C_FN + kc:C_FN + kc + 1], rsq_[:, 0:n], ALU.mult, ALU.mult,
                                [B_xcq, B_cst, B_rsq_], [B_xcq])
                        b_ = Buf("y%d_%d" % (bi_f, c))
                        out_bufs.append(b_)
                        dma("sync", yT[bi_f].rearrange("k p n -> p k n")[:, :, off_c - 256:off_c - 256 + n], xcq[:, :, 0:n], [B_xcq], [b_])

                    merged = []
                    gap_ = max(1, len(stages) // 5)
                    for pos_, st_ in enumerate(stages):
                        if pos_ % gap_ == gap_ // 2 and 1 + pos_ // gap_ <= 4:
                            merged.append((fin_stage2, (1 + pos_ // gap_,)))
                        merged.append(st_)
                    stages = merged
                stage_pos = [0]

                def pump(n=1):
                    for _ in range(n):
                        if stage_pos[0] < len(stages):
                            f_, a_ = stages[stage_pos[0]]
                            stage_pos[0] += 1
                            f_(*a_)
                            if f_ is ml_reset:
                                pump(1)

                n_steps_total = sum((2 if c == 0 else NT) * 8 for c in chunks)
                n_stage = len(stages)
                done_steps = 0
                for c in chunks:
                    off_c, n = CH[c]
                    nqt = n // 128
                    keyt = [0, 1] if c == 0 else list(range(NT))
                    steps = [(h, ki) for h in range(8) for ki in range(len(keyt))]

                    def emit_st(si):
                        h, ki = steps[si]
                        j, base, kvh = h // 2, (h % 2) * 64, h // 4
                        kt = keyt[ki]
                        sb = si % 2
                        mm(ps[sb][:, 0:n], kTz[:, kvh, h % 2, kt * 128:(kt + 1) * 128], qT[:, j, off_c:off_c + n],
                           True, True, [B_kTd[kt], B_qT[c]], [PB[sb]])

                    emit_st(0)
                    for si, (h, ki) in enumerate(steps):
                        kvh = h // 4
                        kt = keyt[ki]
                        sb = si % 2
                        if si + 1 < len(steps):
                            emit_st(si + 1)
                        act(PT[sb][:, 0:n], ps[sb][:, 0:n], AF.Exp, [PB[sb]], [B_PT[sb]], scale=0.125)
                        for qt in range(nqt):
                            mm(ps[2][:, qt * 65:(qt + 1) * 65], PT[sb][:, qt * 128:(qt + 1) * 128], Vaug[:, kt, kvh, :],
                               ki == 0 and qt == 0, ki == len(keyt) - 1, [B_PT[sb], B_V[kt]], [PB[2]], skip=True)
                        if ki == len(keyt) - 1:
                            O3 = ps[2][:, 0:nqt * 65].rearrange("p (q c) -> p q c", c=65)
                            recip(rinv[:, 0:nqt], O3[:, :, 64], [PB[2]], [B_rinv])
                            tt("vector", att_tok[:, 0:nqt, h * 64:(h + 1) * 64], O3[:, :, 0:64],
                               rinv[:, 0:nqt].unsqueeze(2).to_broadcast([128, nqt, 64]), ALU.mult, [PB[2], B_rinv], [B_att])
                        done_steps += 1
                        target = (done_steps * n_stage) // n_steps_total
                        if target > stage_pos[0] and (done_steps % 3 == 0 or c == 0):
                            pump(1)
                    if bi == 0 and l == 0 and c == 1:
                        dump("att_tok", lambda: att_tok[:, 0, :], [B_att])
                    pT = psb(0)
                    for qt in range(nqt):
                        for jj in range(4):
                            tr(pT[:, jj * 128:(jj + 1) * 128], att_tok[:, qt, jj * 128:(jj + 1) * 128], ident_bf, [B_att, B_ident], [PB[0]])
                        act(qT[:, :, off_c + qt * 128:off_c + (qt + 1) * 128], pT[:, 0:512].rearrange("p (j n) -> p j n", j=4), AF.Copy,
                            [PB[0]], [B_qT[c]])
                pump(len(stages))
                if bi == 0 and l == 0:
                    dump("Hsum", lambda: Hsum[:, 2].rearrange("p h d -> p (h d)"), [B_H[2]])
                stop('B2')
                for c in chunks:
                    off_c, n = CH[c]
                    W = n + 16
                    upos = 8 if c == 0 else off_c + 24
                    til = tiles_of(c)
                    cp("vector", z32[:, :, 0:W], uT[:, :, upos - 8:upos + n + 8], B_uT + [B_uTall], [B_z])
                    lvl_src = z32
                    bufs = [(pX, B_pX), (pY, B_pY)]
                    B_src_l = B_z
                    for lv in range(4):
                        sh = 1 << lv if lv > 0 else 1
                        dst, B_dst = bufs[lv % 2]
                        if lv == 0:
                            tt("gpsimd", dst[:, :, 1:W], z32[:, :, 1:W], z32[:, :, 0:W - 1], ALU.add, [B_z], [B_dst])
                        else:
                            s = 1 << (lv - 1)
                            tt("gpsimd", dst[:, :, s:W - s], lvl_src[:, :, 0:W - 2 * s], lvl_src[:, :, 2 * s:W], ALU.add, [B_src_l], [B_dst])
                        lvl_src, B_src_l = dst, B_dst
                        w = 2 << lv
                        jg, rows = lv // 2, slice((lv % 2) * 64, (lv % 2) * 64 + 64)
                        stt("vector", ppT[rows, jg, 0:n], dst[rows, jg, 8:8 + n], 1.0 / w, z32[rows, jg, 8:8 + n], ALU.mult, ALU.subtract,
                            [B_dst, B_z], [B_ppT])
                        edges = []
                        if c in (0, 1):
                            edges.append((0, 0))
                        if c in (0, 4):
                            edges.append((1, n - 8))
                        for (side, e0) in edges:
                            ec = cst[rows, C_EDGE + jg * 16 + side * 8:C_EDGE + jg * 16 + side * 8 + 8]
                            tt("vector", etmp[rows, :], dst[rows, jg, 8 + e0:16 + e0], ec, ALU.mult, [B_dst, B_cst], [B_et])
                            tt("vector", ppT[rows, jg, e0:e0 + 8], etmp[rows, :], z32[rows, jg, 8 + e0:16 + e0], ALU.subtract,
                               [B_et, B_z], [B_ppT])
                    for j in range(2):
                        mm(ps[3][:, 0:n], pw_sb[:, l, j, :], ppT[:, j, 0:n], True, True, [B_pw, B_ppT], [PB[3]])
                        ts("vector", mkT[:, j, off_c:off_c + n], ps[3][:, 0:n], lay(l, L_PSC, 2)[:, j:j + 1], None, ALU.mult, None,
                           [PB[3], B_cst], [B_mkT[t] for t in til])
                if bi == 0 and l == 0:
                    dump("mix", lambda: qT[:, :, 256:384], B_qT)
                    dump("mixml", lambda: mqT[:, :, 256:384], B_mqT)
                    dump("mixpool", lambda: mkT[:, :, 256:384], B_mkT)
                S.barrier()
                stop('B')

                H2_OFF = WS_OFF
                h2T = R(H2_OFF, 8 * NTOK, BF16).rearrange("p (k n) -> p k n", k=8)
                B_h2T = [Buf("h2T%d" % c) for c in range(5)]
                o = OV0
                wo = R(o, 8 * 1024, BF16).rearrange("p (k n) -> p k n", k=8); o += 4096
                xcC = [R(o + i * 4096, 8 * 512).rearrange("p (k n) -> p k n", k=8) for i in range(2)]; o += 8192
                sqr = [R(o + i * 256, 512, BF16) for i in range(2)]; o += 512
                rstd = R(o, 512); o += 512
                tmpAA = [R(o + i * 512, 512) for i in range(2)]; o += 1024
                lnr = tmpAA[1]
                assert o <= H2_OFF, (o, H2_OFF)
                B_wo, B_lnr, B_rstd = Buf("wo"), Buf("lnr"), Buf("rstd")
                B_xcC = [Buf("xcC0"), Buf("xcC1")]
                B_sqr = [Buf("sqr0"), Buf("sqr1")]
                B_tmpAA = [Buf("tmpA0"), Buf("tmpA1")]
                for kc in range(8):
                    dma("gpsimd", wo[:, kc, :], w_out[l].rearrange("(k p) n -> p k n", p=128)[:, kc, :], [], [B_wo])
                wgs = [R(AB0 + i * 2048, 8 * 256, BF16).rearrange("p (k n) -> p k n", k=8) for i in range(2)]
                wus = [R(AB0 + 1024 + i * 2048, 8 * 256, BF16).rearrange("p (k n) -> p k n", k=8) for i in range(2)]
                B_wgs = [Buf("wg0"), Buf("wg1")]
                B_wus = [Buf("wu0"), Buf("wu1")]

                def load_slab(f):
                    s_ = (f // 2) % 2
                    dma("gpsimd", wgs[s_], w_g[l].rearrange("(k p) n -> p k n", p=128)[:, :, f * 128:(f + 2) * 128], [], [B_wgs[s_]])
                    dma("gpsimd", wus[s_], w_u[l].rearrange("(k p) n -> p k n", p=128)[:, :, f * 128:(f + 2) * 128], [], [B_wus[s_]])

                load_slab(0)
                load_slab(2)
                bankc = [0]

                def loadC(i):
                    c = chunks[i]
                    off_c, n = CH[c]
                    dma("sync", xcC[i % 2][:, :, 0:n], src[bi].rearrange("k p n -> p k n")[:, :, off_c:off_c + n], [B_src[c]], [B_xcC[i % 2]])

                def projC(i):
                    c = chunks[i]
                    off_c, n = CH[c]
                    r = 2 if c == 0 else bi
                    til = tiles_of(c)
                    xc, B_xc = xcC[i % 2], B_xcC[i % 2]
                    mix_reads = [B_qT[c]] + [B_mqT[t] for t in til] + [B_mkT[t] for t in til]
                    for oc in range(8):
                        bank = bankc[0] % 4
                        bankc[0] += 1
                        for kc in range(8):
                            mm(ps[bank][:, 0:n], wo[:, kc, oc * 128:(oc + 1) * 128], mixT(kc)[:, off_c:off_c + n], kc == 0, kc == 7,
                               [B_wo] + mix_reads, [PB[bank]])
                        stt("vector", xc[:, oc, 0:n], ps[bank][:, 0:n], modcol(l, r, 16 + oc), xc[:, oc, 0:n], ALU.mult, ALU.add,
                            [PB[bank], B_mods, B_xc], [B_xc])
                    dma("sync", xs2.rearrange("k p n -> p k n")[:, :, off_c:off_c + n], xc[:, :, 0:n], [B_xc], [B_xs2[c]])
                    if bi == 0 and l == 0 and c == 1:
                        dump("x2", lambda: xc[:, 0, :], [B_xc])

                def normC(i):
                    c = chunks[i]
                    off_c, n = CH[c]
                    r = 2 if c == 0 else bi
                    xc, B_xc = xcC[i % 2], B_xcC[i % 2]
                    rms_rstd(xc, n, sqr, lnr, rstd, B_xc, B_sqr, B_lnr, B_rstd, 4)
                    for kc in range(8):
                        tA, B_tA = tmpAA[kc % 2], B_tmpAA[kc % 2]
                        stt("vector", tA[:, 0:n], xc[:, kc, 0:n], modcol(l, r, 32 + kc), rstd[:, 0:n], ALU.mult, ALU.mult,
                            [B_xc, B_mods, B_rstd], [B_tA])
                        act(h2T[:, kc, off_c:off_c + n], tA[:, 0:n], AF.Identity, [B_tA, B_mods], [B_h2T[c]], bias=modcol(l, r, 24 + kc))

                loadC(0)
                if len(chunks) > 1:
                    loadC(1)
                projC(0)
                for i in range(len(chunks)):
                    if i + 1 < len(chunks):
                        projC(i + 1)
                    normC(i)
                    if i + 2 < len(chunks):
                        loadC(i + 2)
                stop('C')

                aT = R(AB0 + 4096, NF * NTOK, BF16).rearrange("p (f n) -> p f n", f=NF)
                o = AB0 + 4096 + NF * NTOK // 2
                sg = [R(o, 512), R(o + 512, 512)]; o += 1024
                D2_OFF = o
                assert o <= H2_OFF
                B_aT = [[Buf("aT%d_%d" % (f, c)) for c in range(5)] for f in range(NF)]
                B_sg = [Buf("sg0"), Buf("sg1")]
                wds = [R(D2_OFF + i * (NF * 128), NF * 256, BF16).rearrange("p (f n) -> p f n", f=NF) for i in range(2)]
                B_wds = [Buf("wd0"), Buf("wd1")]

                def load_wd(oc, extra_r=()):
                    dma("gpsimd", wds[(oc // 2) % 2], w_d[l].rearrange("(f p) n -> p f n", p=128)[:, :, oc * 128:(oc + 2) * 128],
                        list(extra_r), [B_wds[(oc // 2) % 2]])

                k = 0
                for f in range(NF):
                    s_ = (f // 2) % 2
                    fo_ = (f % 2) * 128
                    if f % 2 == 0 and f >= 4:
                        load_slab(f)
                    if f == 16:
                        load_wd(0, B_xs2)
                        load_wd(2, B_xs2)
                    for c in chunks:
                        off_c, n = CH[c]
                        bg_, bu_ = (0, 1) if k % 2 == 0 else (2, 3)
                        sgi = k % 2
                        k += 1
                        for kc in range(8):
                            mm(ps[bg_][:, 0:n], wgs[s_][:, kc, fo_:fo_ + 128], h2T[:, kc, off_c:off_c + n], kc == 0, kc == 7, [B_wgs[s_], B_h2T[c]], [PB[bg_]])
                        for kc in range(8):
                            mm(ps[bu_][:, 0:n], wus[s_][:, kc, fo_:fo_ + 128], h2T[:, kc, off_c:off_c + n], kc == 0, kc == 7, [B_wus[s_], B_h2T[c]], [PB[bu_]])
                        act(sg[sgi][:, 0:n], ps[bg_][:, 0:n], AF.Silu, [PB[bg_]], [B_sg[sgi]])
                        tt("vector", aT[:, f, off_c:off_c + n], sg[sgi][:, 0:n], ps[bu_][:, 0:n], ALU.mult, [B_sg[sgi], PB[bu_]], [B_aT[f][c]])
                S.barrier(engs=["sync"])

                o = D2_OFF
                o += 2 * NF * 128
                x2o = [R(o, 512), R(o + 512, 512)]; o += 1024
                x3 = [R(o, 512), R(o + 512, 512)]; o += 1024
                assert o <= WS_OFF, (o, WS_OFF)
                B_x2o = [Buf("x2o0"), Buf("x2o1")]
                B_x3 = [Buf("x3_0"), Buf("x3_1")]
                k = 0
                nxt = (bi, l + 1) if l + 1 < n_layers else ((bi + 1, 0) if bi + 1 < n_b else None)
                for oc in range(8):
                    s_ = (oc // 2) % 2
                    oo_ = (oc % 2) * 128
                    if oc % 2 == 0 and oc >= 4:
                        load_wd(oc)
                    if oc == 2 and nxt is not None:
                        load_win(nxt[1], B_h2T)
                        win_state["loaded_for"] = nxt
                    for c in chunks:
                        off_c, n = CH[c]
                        r = 2 if c == 0 else bi
                        bank = k % 4
                        i2 = k % 2
                        k += 1
                        dma("sync", x2o[i2][:, 0:n], xs2[oc][:, off_c:off_c + n], [B_xs2[c]], [B_x2o[i2]])
                        for f in range(NF):
                            mm(ps[bank][:, 0:n], wds[s_][:, f, oo_:oo_ + 128], aT[:, f, off_c:off_c + n], f == 0, f == NF - 1, [B_wds[s_], B_aT[f][c]], [PB[bank]])
                        stt("vector", x3[i2][:, 0:n], ps[bank][:, 0:n], modcol(l, r, 40 + oc), x2o[i2][:, 0:n], ALU.mult, ALU.add,
                            [PB[bank], B_mods, B_x2o[i2]], [B_x3[i2]])
                        dma("sync", xs[bi][oc][:, off_c:off_c + n], x3[i2][:, 0:n], [B_x3[i2]], [B_xs[bi][c]])
                S.barrier()

            if bi == n_b - 1:
                o = OV0
                xcf = [R(o + i * 4096, 8 * 512).rearrange("p (k n) -> p k n", k=8) for i in range(2)]; o += 8192
                yof = [R(o + i * 4096, 8 * 512).rearrange("p (k n) -> p k n", k=8) for i in range(2)]; o += 8192
                sqr = [R(o + i * 256, 512, BF16) for i in range(2)]; o += 512
                lnr = R(o, 512); o += 512
                rstd = R(o, 512); o += 512
                B_xcf = [Buf("xcf0"), Buf("xcf1")]
                B_yof = [Buf("yof0"), Buf("yof1")]
                B_lnr, B_rstd = Buf("lnr"), Buf("rstd")
                B_sqr = [Buf("sqr0"), Buf("sqr1")]
                def load_f(c):
                    off_c, n = CH[c]
                    dma("sync", xcf[c % 2][:, :, 0:n], xs[bi].rearrange("k p n -> p k n")[:, :, off_c:off_c + n], [B_xs[bi][c]], [B_xcf[c % 2]])
                load_f(1)
                load_f(2)
                for c in range(1, 5):
                    off_c, n = CH[c]
                    xc, yo, B_xc, B_yo = xcf[c % 2], yof[c % 2], B_xcf[c % 2], B_yof[c % 2]
                    rms_rstd(xc, n, sqr, lnr, rstd, B_xc, B_sqr, B_lnr, B_rstd, c % 2)
                    for kc in range(8):
                        stt("vector", yo[:, kc, 0:n], xc[:, kc, 0:n], cst[:, C_FN + kc:C_FN + kc + 1], rstd[:, 0:n], ALU.mult, ALU.mult,
                            [B_xc, B_cst, B_rstd], [B_yo])
                    b = Buf("y%d_%d" % (bi, c))
                    out_bufs.append(b)
                    dma("sync", yT[bi].rearrange("k p n -> p k n")[:, :, off_c - 256:off_c - 256 + n], yo[:, :, 0:n], [B_yo], [b])
                    if c + 2 < 5:
                        load_f(c + 2)
                S.barrier()
            else:
                fin_todo.append(bi)
    except _Stop:
        pass

    S.op("sync", lambda e: e.nop(), out_bufs, [])
    S.emit(nc, st)
    st.close()
    return nc, S


def _rope_tables(rows=32, grid_w=64, head_dim=64, theta=10000.0):
    half = head_dim // 2
    inv_freq = (1.0 / (np.float32(theta) ** (np.arange(0, half, 2, dtype=np.float32) / np.float32(half)))).astype(np.float32)
    row = np.repeat(np.arange(rows, dtype=np.float32), grid_w)
    col = np.tile(np.arange(grid_w, dtype=np.float32), rows)
    a_row = row[:, None] * inv_freq
    a_col = col[:, None] * inv_freq
    ang = np.concatenate([a_row, a_row, a_col, a_col], -1).astype(np.float32)
    return np.cos(ang).astype(np.float32), np.sin(ang).astype(np.float32)


def _const_blob():
    blob = np.zeros((128, CW), np.float32)
    idx = np.arange(128)
    blob[:, C_IDENT:C_IDENT + 128] = np.eye(128, dtype=np.float32)
    blob[:, C_NEGUF:C_NEGUF + 128] = -(idx[:, None] <= idx[None, :]).astype(np.float32)
    blob[:, C_NEGUB:C_NEGUB + 128] = -(idx[:, None] >= idx[None, :]).astype(np.float32)
    blob[:, C_MNF:C_MNF + 128] = NEG * (idx[:, None] > idx[None, :]).astype(np.float32)
    blob[:, C_MNB:C_MNB + 128] = NEG * (idx[:, None] < idx[None, :]).astype(np.float32)
    blob[:, C_NEGONES:C_NEGONES + 128] = -1.0
    cos, sin = _rope_tables()
    sgn = np.ones(64, np.float32)
    sgn[0:16] = -1.0
    sgn[32:48] = -1.0
    sinS = sin * sgn[None, :]
    blob[:, C_COS:C_COS + 1024] = cos.reshape(16, 128, 64).transpose(1, 0, 2).reshape(128, 1024)
    blob[:, C_SIN:C_SIN + 1024] = sinS.reshape(16, 128, 64).transpose(1, 0, 2).reshape(128, 1024)
    wins = (2, 4, 8, 16)
    for p in range(128):
        for j in range(2):
            w = wins[2 * j + (1 if p >= 64 else 0)]
            for e in range(8):
                t = e
                cnt_l = (t + w // 2) - max(t - w // 2, 0)
                k = 8 - e
                cnt_r = min(w // 2, k) + w // 2
                blob[p, C_EDGE + j * 16 + e] = 1.0 / cnt_l
                blob[p, C_EDGE + j * 16 + 8 + e] = 1.0 / cnt_r
    return blob


def _fm(v, nchunk):
    return np.ascontiguousarray(np.asarray(v, np.float32).reshape(nchunk, 128).T)


_NC_CACHE = {}


def prep_in_maps(x, c, ctx, c_ctx, w_ada, b_ada, norm_mix, w_in, b_gates, q_norm, k_norm, ml_norm, pool_w, pool_scale, w_out,
                 norm_ffn, w_ffn_gate, w_ffn_up, w_ffn_down, final_norm):
    f = lambda a: np.ascontiguousarray(np.asarray(a, dtype=np.float32))
    x, c, ctx, c_ctx = f(x), f(c), f(ctx), f(c_ctx)
    n_cores = 8
    base = _const_blob()
    for l in range(2):
        o = C_LAY + l * LAYW
        base[:, o + L_BADA:o + L_BADA + 48] = _fm(b_ada[l], 48)
        base[:, o + L_NMIX:o + L_NMIX + 8] = _fm(norm_mix[l], 8)
        base[:, o + L_NFFN:o + L_NFFN + 8] = _fm(norm_ffn[l], 8)
        base[:, o + L_MLG:o + L_MLG + 2] = _fm(ml_norm[l], 2)
        base[:, o + L_PSC:o + L_PSC + 2] = _fm(pool_scale[l], 2)
        base[:, o + L_QG:o + L_QG + 64] = np.asarray(q_norm[l], np.float32)[None, :]
        base[:, o + L_KG:o + L_KG + 64] = np.asarray(k_norm[l], np.float32)[None, :]
        base[:, o + L_BG:o + L_BG + 16] = np.asarray(b_gates[l], np.float32)[None, :]
    base[:, C_FN:C_FN + 8] = _fm(final_norm, 8)
    pw = np.asarray(pool_w, np.float32)
    pwbd = np.zeros((2, 2, 128, 128), np.float32)
    for l in range(2):
        for j in range(2):
            pwbd[l, j, 0:64, 0:64] = pw[l, 2 * j]
            pwbd[l, j, 64:128, 64:128] = pw[l, 2 * j + 1]
    shared = {"pwbd": pwbd, "w_ada": f(w_ada), "w_in": f(w_in), "w_out": f(w_out), "w_g": f(w_ffn_gate), "w_u": f(w_ffn_up),
              "w_d": f(w_ffn_down)}
    in_maps = []
    for i in range(n_cores):
        seq = np.concatenate([ctx[2 * i:2 * i + 2], x[2 * i:2 * i + 2]], axis=1)
        xin = np.ascontiguousarray(seq.transpose(0, 2, 1)).reshape(2, 8, 128, NTOK)
        blob = base.copy()
        cs = np.stack([c[2 * i], c[2 * i + 1], c_ctx], axis=-1)
        blob[:, C_CSEL:C_CSEL + 24] = cs.reshape(8, 128, 3).transpose(1, 0, 2).reshape(128, 24)
        m = {"xin": xin, "cst": blob}
        m.update(shared)
        in_maps.append(m)
    return in_maps


def kernel(**inputs):
    n_cores = 8
    in_maps = prep_in_maps(**inputs)
    if "nc" not in _NC_CACHE:
        _NC_CACHE["nc"] = build_nc()[0]
    nc = _NC_CACHE["nc"]
    res = run_bass_kernel_spmd(nc, in_maps, core_ids=list(range(n_cores)))
    out = np.empty((16, 2048, 1024), np.float32)
    for i in range(n_cores):
        y = np.asarray(res.results[i]["yT"]).reshape(2, 1024, 2048)
        out[2 * i:2 * i + 2] = y.transpose(0, 2, 1)
    return out
```

```python
import numpy as np
from contextlib import ExitStack
import concourse.bass as bass
import concourse.mybir as mybir
from concourse.bass_utils import run_bass_kernel_spmd

F32 = mybir.dt.float32
BF16 = mybir.dt.bfloat16
ALU = mybir.AluOpType
AF = mybir.ActivationFunctionType
AX = mybir.AxisListType

ENGS = ["sync", "scalar", "vector", "gpsimd", "tensor"]

D = 1024
NTOK = 2304
NT = 18
DFF = 2816
NF = 22
INW = 2064
CH = [(0, 256), (256, 512), (768, 512), (1280, 512), (1792, 512)]
EPS = 1e-6
NEG = -30000.0

C_IDENT = 0
C_NEGUF = 128
C_NEGUB = 256
C_MNF = 384
C_MNB = 512
C_NEGONES = 640
C_COS = 768
C_SIN = 1792
C_LAY = 2816
L_BADA, L_NMIX, L_NFFN, L_MLG, L_PSC, L_QG, L_KG, L_BG = 0, 48, 56, 64, 66, 68, 132, 196
LAYW = 212
C_FN = C_LAY + 2 * LAYW
C_CSEL = C_FN + 8
C_EDGE = C_CSEL + 24
CW = C_EDGE + 32


class Buf:
    __slots__ = ("name", "w", "r", "rd")

    def __init__(self, name=""):
        self.name = name
        self.w = None
        self.r = {}
        self.rd = []


class Op:
    __slots__ = ("eng", "pos", "fn", "deps", "is_dma", "dsem", "dval", "signal", "sigval", "prewait")


class Sched:
    def __init__(self, n_dma=16):
        self.ops = {e: [] for e in ENGS}
        self.seen = {e: {} for e in ENGS}
        self.n_dma = n_dma
        self.pool_base = {"sync": 0, "gpsimd": n_dma, "scalar": 2 * n_dma}
        self.dma_tot = [0] * (3 * n_dma)
        self.dma_rr = {"sync": 0, "gpsimd": 0, "scalar": 0}
        self.pending = {e: [] for e in ENGS}
        self.dma_since_bar = []

    def _add_dep(self, o, d, deps):
        if d.is_dma:
            key = ("d", d.dsem)
            val = d.dval
        else:
            key = ("e", d.eng)
            val = d.pos
        if self.seen[o.eng].get(key, 0) >= val:
            return
        self.seen[o.eng][key] = val
        deps.append(d)

    def op(self, eng, fn, reads=(), writes=(), dma=False):
        o = Op()
        o.eng = eng
        o.fn = fn
        o.is_dma = dma
        o.signal = False
        o.sigval = 0
        o.prewait = None
        o.dsem = -1
        o.dval = 0
        o.pos = len(self.ops[eng]) + 1
        cand = []
        for b in reads:
            if b.w is not None:
                cand.append((b.w, True))
        for b in writes:
            if b.w is not None:
                cand.append((b.w, False))
            for r in b.r.values():
                cand.append((r, False))
            for r in b.rd:
                cand.append((r, False))
        deps = []
        if self.pending[eng]:
            for d in self.pending[eng]:
                self._add_dep(o, d, deps)
            self.pending[eng] = []
        for d, raw in cand:
            if (not d.is_dma) and (not dma) and d.eng == eng:
                if eng == "tensor":
                    continue
                if not raw:
                    continue
            self._add_dep(o, d, deps)
        if dma:
            base = self.pool_base[eng]
            k = base + self.dma_rr[eng]
            self.dma_rr[eng] = (self.dma_rr[eng] + 1) % self.n_dma
            prev = self.dma_tot[k]
            if prev > 0 and self.seen[eng].get(("d", k), 0) < prev:
                o.prewait = (k, prev)
                self.seen[eng][("d", k)] = prev
            self.dma_tot[k] = prev + 16
            o.dsem = k
            o.dval = prev + 16
            self.dma_since_bar.append(o)
        for d in deps:
            d.signal = True
        o.deps = deps
        self.ops[eng].append(o)
        for b in reads:
            if dma:
                b.rd.append(o)
            else:
                b.r[eng] = o
        for b in writes:
            b.w = o
            b.r = {}
            b.rd = []
        return o

    def barrier(self, engs=None):
        L = []
        for e in ENGS:
            for o in reversed(self.ops[e]):
                if not o.is_dma:
                    L.append(o)
                    break
        L.extend(self.dma_since_bar)
        if engs is None:
            self.dma_since_bar = []
        for e in (ENGS if engs is None else engs):
            self.pending[e] = self.pending[e] + L

    def emit(self, nc, stack):
        sem_e = {e: stack.enter_context(nc.semaphore("se_" + e)) for e in ENGS}
        sem_d = [stack.enter_context(nc.semaphore("sd_%d" % i)) for i in range(3 * self.n_dma)]
        for e in ENGS:
            c = 0
            for o in self.ops[e]:
                if (not o.is_dma) and o.signal:
                    c += 1
                    o.sigval = c

        def run(e):
            def body(engine):
                for o in self.ops[e]:
                    for d in o.deps:
                        if d.is_dma:
                            engine.wait_ge(sem_d[d.dsem], d.dval)
                        else:
                            engine.wait_ge(sem_e[d.eng], d.sigval)
                    if o.prewait is not None:
                        engine.wait_ge(sem_d[o.prewait[0]], o.prewait[1])
                    ins = o.fn(engine)
                    if o.is_dma:
                        ins.then_inc(sem_d[o.dsem], 16)
                    elif o.signal:
                        ins.then_inc(sem_e[e], 1)
            return body

        with nc.Block() as block:
            block.sync(run("sync"))
            block.scalar(run("scalar"))
            block.vector(run("vector"))
            block.gpsimd(run("gpsimd"))
            block.tensor(run("tensor"))


ARENA_W = 53200


class _Stop(Exception):
    pass


def build_nc(n_b=2, n_layers=2, dbg=None, stop_after=None):
    nc = bass.Bass("TRN2", target_bir_lowering=False)
    xin = nc.dram_tensor("xin", [2, 8, 128, NTOK], F32, kind="ExternalInput").ap()
    cstd = nc.dram_tensor("cst", [128, CW], F32, kind="ExternalInput").ap()
    pwbd = nc.dram_tensor("pwbd", [2, 2, 128, 128], F32, kind="ExternalInput").ap()
    w_ada = nc.dram_tensor("w_ada", [2, D, 6 * D], F32, kind="ExternalInput").ap()
    w_in = nc.dram_tensor("w_in", [2, D, INW], F32, kind="ExternalInput").ap()
    w_out = nc.dram_tensor("w_out", [2, D, D], F32, kind="ExternalInput").ap()
    w_g = nc.dram_tensor("w_g", [2, D, DFF], F32, kind="ExternalInput").ap()
    w_u = nc.dram_tensor("w_u", [2, D, DFF], F32, kind="ExternalInput").ap()
    w_d = nc.dram_tensor("w_d", [2, DFF, D], F32, kind="ExternalInput").ap()
    yT = nc.dram_tensor("yT", [2, 8, 128, 2048], F32, kind="ExternalOutput").ap()
    xs = nc.dram_tensor("xs", [2, 8, 128, NTOK], F32, kind="Internal").ap()
    xs2 = nc.dram_tensor("xs2", [8, 128, NTOK], F32, kind="Internal").ap()
    dbg_t = {}
    if dbg:
        for name, shape in dbg.items():
            dbg_t[name] = nc.dram_tensor("dbg_" + name, list(shape), F32, kind="ExternalOutput").ap()

    S = Sched()
    st = ExitStack()
    arena = st.enter_context(nc.sbuf_tensor("arena", [128, ARENA_W], F32))
    ps = [st.enter_context(nc.psum_tensor("ps%d" % i, [128, 512], F32)) for i in range(8)]
    PB = [Buf("ps%d" % i) for i in range(8)]

    def R(off, n, dt=F32):
        if dt == F32:
            assert off + n <= ARENA_W, (off, n)
            return arena[:, off:off + n]
        nw = (n + 1) // 2
        assert off + nw <= ARENA_W, (off, n)
        return arena[:, off:off + nw].bitcast(BF16)[:, 0:n]

    def psb(i):
        return ps[i][:, :].bitcast(BF16)

    def mm(out, lhsT, rhs, start, stop, reads, writes, skip=False):
        if skip:
            S.op("tensor", lambda e: e.matmul(out, lhsT=lhsT, rhs=rhs, start=start, stop=stop, skip_group_check=True), reads, writes)
        else:
            S.op("tensor", lambda e: e.matmul(out, lhsT=lhsT, rhs=rhs, start=start, stop=stop), reads, writes)

    def tr(out, in_, ident, reads, writes):
        S.op("tensor", lambda e: e.transpose(out, in_, ident), reads, writes)

    def act(out, in_, func, reads, writes, bias=None, scale=None):
        kw = {}
        if bias is not None:
            kw["bias"] = bias
        if scale is not None:
            kw["scale"] = scale
        S.op("scalar", lambda e: e.activation(out=out, in_=in_, func=func, **kw), reads, writes)

    def tt(eng, out, in0, in1, op, reads, writes):
        S.op(eng, lambda e: e.tensor_tensor(out=out, in0=in0, in1=in1, op=op), reads, writes)

    def ts(eng, out, in0, s1, s2, op0, op1, reads, writes):
        if s2 is None:
            S.op(eng, lambda e: e.tensor_scalar(out=out, in0=in0, scalar1=s1, scalar2=None, op0=op0), reads, writes)
        else:
            S.op(eng, lambda e: e.tensor_scalar(out=out, in0=in0, scalar1=s1, scalar2=s2, op0=op0, op1=op1), reads, writes)

    def stt(eng, out, in0, scalar, in1, op0, op1, reads, writes):
        S.op(eng, lambda e: e.scalar_tensor_tensor(out=out, in0=in0, scalar=scalar, in1=in1, op0=op0, op1=op1), reads, writes)

    def cp(eng, out, in_, reads, writes):
        S.op(eng, lambda e: e.tensor_copy(out=out, in_=in_), reads, writes)

    def red(out, in_, reads, writes):
        S.op("vector", lambda e: e.tensor_reduce(out=out, in_=in_, axis=AX.X, op=ALU.add), reads, writes)

    def recip(out, in_, reads, writes):
        S.op("vector", lambda e: e.reciprocal(out=out, in_=in_), reads, writes)

    def mset(eng, ap, val, writes):
        S.op(eng, lambda e: e.memset(ap, val), (), writes)

    def dma(eng, out, in_, reads, writes):
        return S.op(eng, lambda e: e.dma_start(out=out, in_=in_), reads, writes, dma=True)

    out_bufs = []

    def dump(name, apf, reads):
        if dbg and name in dbg_t:
            b = Buf("dbg_" + name)
            out_bufs.append(b)
            dma("gpsimd", dbg_t[name], apf(), reads, [b])

    off = 0
    cst = R(off, CW); off += CW
    pw_sb = R(off, 2 * 2 * 128, BF16).rearrange("p (l j n) -> p l j n", l=2, j=2); off += 256
    ident_bf = R(off, 128, BF16); off += 64
    ones_bf = R(off, 128, BF16); off += 64
    mods = R(off, 2 * 3 * 48).rearrange("p (l r m) -> p l r m", l=2, r=3); off += 288
    Cst = R(off, 260).rearrange("p (j n) -> p j n", j=2); off += 260
    CTb = R(off, 260, BF16).rearrange("p (j n) -> p j n", j=2); off += 130
    scT = R(off, 24, BF16).rearrange("p (k r) -> p k r", r=3); off += 12
    cbf = R(off, 768, BF16); off += 384
    off = (off + 7) // 8 * 8
    P_END = off
    B_cst, B_pw, B_ident, B_ones, B_mods, B_C, B_CTb, B_scT = (Buf(n) for n in
                                                                ["cst", "pw", "identb", "onesb", "mods", "C", "CTb", "scT"])

    ident_f = cst[:, C_IDENT:C_IDENT + 128]
    negones_b = cbf[:, C_NEGONES:C_NEGONES + 128]
    negU = [cbf[:, C_NEGUF:C_NEGUF + 128], cbf[:, C_NEGUB:C_NEGUB + 128]]
    Mn = [cbf[:, C_MNF:C_MNF + 128], cbf[:, C_MNB:C_MNB + 128]]
    cos_t = cst[:, C_COS:C_COS + 1024].rearrange("p (t d) -> p t d", d=64)
    sin_t = cst[:, C_SIN:C_SIN + 1024].rearrange("p (t d) -> p t d", d=64)

    def lay(l, o, n):
        return cst[:, C_LAY + l * LAYW + o:C_LAY + l * LAYW + o + n]

    AB0 = P_END
    off = AB0
    kTz = R(off, 4 * NTOK, BF16).rearrange("p (j u n) -> p j u n", j=2, u=2); off += 2 * NTOK
    qT = R(off, 4 * NTOK, BF16).rearrange("p (j n) -> p j n", j=4); off += 2 * NTOK
    mqT = R(off, 2 * NTOK, BF16).rearrange("p (j n) -> p j n", j=2); off += NTOK
    mkT = R(off, 2 * NTOK, BF16).rearrange("p (j n) -> p j n", j=2); off += NTOK
    Vaug = R(off, NT * 130, BF16).rearrange("p (t h c) -> p t h c", t=NT, h=2); off += NT * 65
    mk_tok = R(off, NT * 256, BF16).rearrange("p (t c) -> p t c", t=NT); off += NT * 128
    mVaug = R(off, NT * 260, BF16).rearrange("p (t h c) -> p t h c", t=NT, h=4); off += NT * 130
    moT = R(off, 2 * NTOK, BF16).rearrange("p (j n) -> p j n", j=2); off += NTOK
    G = R(off, NT * 16).rearrange("p (t c) -> p t c", t=NT); off += NT * 16
    UW = 2352
    uT = R(off, 2 * UW, BF16).rearrange("p (j n) -> p j n", j=2); off += UW
    off = (off + 7) // 8 * 8
    OV0 = off

    def mixT(kc):
        if kc < 4:
            return qT[:, kc, :]
        if kc < 6:
            return mqT[:, kc - 4, :]
        return mkT[:, kc - 6, :]

    B_qT = [Buf("qT%d" % c) for c in range(5)]
    B_kTd = [Buf("kTd%d" % t) for t in range(NT)]
    B_V = [Buf("V%d" % t) for t in range(NT)]
    B_mqT = [Buf("mqT%d" % t) for t in range(NT)]
    B_mkT = [Buf("mkT%d" % t) for t in range(NT)]
    B_mk = [Buf("mk%d" % t) for t in range(NT)]
    B_mV = [Buf("mV%d" % t) for t in range(NT)]
    B_moT = [Buf("moT%d" % c) for c in range(5)]
    B_G = [Buf("G%d" % t) for t in range(NT)]
    B_uT = [Buf("uT%d" % c) for c in range(5)]
    B_uTall = Buf("uTall")
    B_xs = [[Buf("xs%d_%d" % (b, c)) for c in range(5)] for b in range(2)]
    B_xs2 = [Buf("xs2_%d" % c) for c in range(5)]

    def tiles_of(c):
        o, n = CH[c]
        return list(range(o // 128, (o + n) // 128))

    dma("sync", cst, cstd, [], [B_cst])
    for l in range(2):
        for j in range(2):
            dma("gpsimd", pw_sb[:, l, j, :], pwbd[l, j], [], [B_pw])
    cp("vector", ident_bf, ident_f, [B_cst], [B_ident])
    cp("vector", cbf, cst[:, 0:768], [B_cst], [B_cst])
    mset("vector", ones_bf, 1.0, [B_ones])
    csel = cst[:, C_CSEL:C_CSEL + 24].rearrange("p (k r) -> p k r", r=3)
    act(scT, csel, AF.Silu, [B_cst], [B_scT])
    def ada_dma(lw, piece, wa, B_wa):
        for kc in range(8):
            dma("gpsimd", wa[:, kc, :], w_ada[lw].rearrange("(k p) n -> p k n", p=128)[:, kc, piece * 1024:(piece + 1) * 1024],
                [], [B_wa])

    def ada_mm(lw, piece, wa, B_wa, bank, col0):
        for m8 in range(8):
            for kc in range(8):
                mm(ps[bank][:, col0 + m8 * 3:col0 + (m8 + 1) * 3], wa[:, kc, m8 * 128:(m8 + 1) * 128], scT[:, kc, :],
                   kc == 0, kc == 7, [B_wa, B_scT], [PB[bank]])
        tt("vector", mods[:, lw, :, piece * 8:(piece + 1) * 8].rearrange("p r m -> p m r"),
           ps[bank][:, col0:col0 + 24].rearrange("p (m r) -> p m r", r=3),
           lay(lw, L_BADA + piece * 8, 8).unsqueeze(2).to_broadcast([128, 8, 3]), ALU.add, [PB[bank], B_cst], [B_mods])
        if piece == 1:
            stt("vector", mods[:, lw, :, 8:16], mods[:, lw, :, 8:16], 1.0,
                lay(lw, L_NMIX, 8).unsqueeze(1).to_broadcast([128, 3, 8]), ALU.add, ALU.mult, [B_mods, B_cst], [B_mods])
        if piece == 4:
            stt("vector", mods[:, lw, :, 32:40], mods[:, lw, :, 32:40], 1.0,
                lay(lw, L_NFFN, 8).unsqueeze(1).to_broadcast([128, 3, 8]), ALU.add, ALU.mult, [B_mods, B_cst], [B_mods])

    wa_sb = R(OV0, 8 * 1024, BF16).rearrange("p (k n) -> p k n", k=8)
    B_wa = Buf("wa")
    for piece in range(2):
        ada_dma(0, piece, wa_sb, B_wa)
        ada_mm(0, piece, wa_sb, B_wa, 0, piece * 24)
    ada_todo = [(0, p_) for p_ in range(2, 6)] + [(l_, p_) for l_ in range(1, n_layers) for p_ in range(6)]
    S.barrier()

    def stop(tag):
        if stop_after == tag:
            raise _Stop()

    def modcol(l, r, m):
        return mods[:, l, r, m:m + 1]

    def rms_rstd(xc, n, sqr, lnr, rstd, B_xc, B_sqr, B_lnr, B_rstd, bank):
        for kc in range(8):
            act(sqr[kc % 2][:, 0:n], xc[:, kc, 0:n], AF.Square, [B_xc], [B_sqr[kc % 2]])
            mm(ps[bank][:, 0:n], ones_bf, sqr[kc % 2][:, 0:n], kc == 0, kc == 7, [B_ones, B_sqr[kc % 2]], [PB[bank]])
        act(lnr[:, 0:n], ps[bank][:, 0:n], AF.Ln, [PB[bank]], [B_lnr], bias=EPS, scale=1.0 / D)
        act(rstd[:, 0:n], lnr[:, 0:n], AF.Exp, [B_lnr], [B_rstd], scale=-0.5)

    WS_OFF = ARENA_W - 8 * NTOK // 2
    win = R(WS_OFF, 8 * INW, BF16).rearrange("p (k n) -> p k n", k=8)
    B_win = Buf("win")
    win_state = {"loaded_for": None}

    def load_win(l_, extra_w=()):
        for kc in range(8):
            dma("gpsimd", win[:, kc, :], w_in[l_].rearrange("(k p) n -> p k n", p=128)[:, kc, :], [], [B_win] + list(extra_w))

    fin_todo = []
    try:
        stop('setup')
        for bi in range(n_b):
            for l in range(n_layers):
                src = xin if l == 0 else xs
                B_src = [Buf("xin%d" % c) for c in range(5)] if l == 0 else B_xs[bi]
                chunks = [0, 1, 2, 3, 4] if l == 0 else [1, 2, 3, 4]

                o = OV0
                xc1 = R(o, 8 * 512).rearrange("p (k n) -> p k n", k=8); o += 4096
                xcA = [xc1, xc1]
                hTA = [R(o + i * 2048, 8 * 512, BF16).rearrange("p (k n) -> p k n", k=8) for i in range(2)]; o += 4096
                sqr = [R(o + i * 256, 512, BF16) for i in range(2)]; o += 512
                rstd = R(o, 512); o += 512
                tmpAA = [R(o + i * 512, 512) for i in range(2)]; o += 1024
                lnr = tmpAA[1]
                qk32 = R(o, 640); o += 640
                t1 = R(o, 640); o += 640
                sq = t1
                t2 = R(o, 640); o += 640
                q_bfs = [R(o + i * 256, 512, BF16) for i in range(2)]; o += 512
                kd_bfs = [R(o + i * 128, 256, BF16) for i in range(2)]; o += 256
                ssq = R(o, 16); o += 16
                lnq = R(o, 16); o += 16
                rq = R(o, 16); o += 16
                assert o <= WS_OFF, (o, WS_OFF)
                B_xc1 = Buf("xc")
                B_xcA = [B_xc1, B_xc1]
                B_hTA = [Buf("hT0"), Buf("hT1")]
                B_sqr = [Buf("sqr0"), Buf("sqr1")]
                B_tmpAA = [Buf("tmpA0"), Buf("tmpA1")]
                B_lnr, B_rstd = Buf("lnr"), Buf("rstd")
                B_qk32, B_t1, B_t2, B_ssq, B_lnq, B_rq = (Buf(n) for n in ["qk32", "t1", "t2", "ssq", "lnq", "rq"])
                B_qbfs = [Buf("qbf0"), Buf("qbf1")]
                B_kdbfs = [Buf("kdbf0"), Buf("kdbf1")]
                late_q = []
                tile_ctr = [0]
                B_sq = B_t1
                if win_state["loaded_for"] != (bi, l):
                    load_win(l)
                    win_state["loaded_for"] = (bi, l)
                mset("gpsimd", Vaug.rearrange("p t h c -> p (t h c)"), 1.0, B_V)
                mset("gpsimd", mVaug.rearrange("p t h c -> p (t h c)"), 1.0, B_mV)
                mset("gpsimd", uT.rearrange("p j n -> p (j n)"), 0.0, B_uT + [B_uTall])
                mset("gpsimd", kTz.rearrange("p j u n -> p (j u n)"), 0.0, B_kTd)

                def prepA(c):
                    off_c, n = CH[c]
                    r = 2 if c == 0 else bi
                    xc, hT, B_xc, B_hT = xcA[c % 2], hTA[c % 2], B_xcA[c % 2], B_hTA[c % 2]
                    dma("sync", xc[:, :, 0:n], src[bi].rearrange("k p n -> p k n")[:, :, off_c:off_c + n], [B_src[c]], [B_xc])
                    rms_rstd(xc, n, sqr, lnr, rstd, B_xc, B_sqr, B_lnr, B_rstd, 0)
                    for kc in range(8):
                        tA, B_tA = tmpAA[kc % 2], B_tmpAA[kc % 2]
                        stt("vector", tA[:, 0:n], xc[:, kc, 0:n], modcol(l, r, 8 + kc), rstd[:, 0:n], ALU.mult, ALU.mult,
                            [B_xc, B_mods, B_rstd], [B_tA])
                        act(hT[:, kc, 0:n], tA[:, 0:n], AF.Identity, [B_tA, B_mods], [B_hT], bias=modcol(l, r, kc))

                prepA(0)
                tok_set = 0
                fm_rot = 0
                for c in range(5):
                    off_c, n = CH[c]
                    r = 2 if c == 0 else bi
                    if c + 1 < 5:
                        prepA(c + 1)
                    hT, B_hT = hTA[c % 2], B_hTA[c % 2]
                    for lt in range(n // 128):
                        tt_i = off_c // 128 + lt
                        bA, bB, bC = (1, 2, 3) if tok_set == 0 else (4, 5, 6)
                        tok_set ^= 1
                        hsl = slice(lt * 128, (lt + 1) * 128)
                        q_bf, kd_bf = q_bfs[tile_ctr[0] % 2], kd_bfs[tile_ctr[0] % 2]
                        B_qbf, B_kdbf = B_qbfs[tile_ctr[0] % 2], B_kdbfs[tile_ctr[0] % 2]
                        tile_ctr[0] += 1
                        for kc in range(8):
                            mm(ps[bA][:, 0:512], hT[:, kc, hsl], win[:, kc, 0:512], kc == 0, kc == 7, [B_hT, B_win], [PB[bA]])
                        for kc in range(8):
                            mm(ps[bB][:, 0:256], hT[:, kc, hsl], win[:, kc, 512:768], kc == 0, kc == 7, [B_hT, B_win], [PB[bB]])
                        for kc in range(8):
                            mm(ps[bB][:, 256:272], hT[:, kc, hsl], win[:, kc, 1792:1808], kc == 0, kc == 7, [B_hT, B_win], [PB[bB]])
                        for kc in range(8):
                            mm(ps[bC][:, 0:512], hT[:, kc, hsl], win[:, kc, 1024:1536], kc == 0, kc == 7, [B_hT, B_win], [PB[bC]])
                        if len(late_q) >= 2:
                            late_q.pop(0)()
                        act(qk32[:, 0:512], ps[bA][:, 0:512], AF.Copy, [PB[bA]], [B_qk32])
                        act(qk32[:, 512:640], ps[bB][:, 0:128], AF.Copy, [PB[bB]], [B_qk32])
                        tt("vector", sq, qk32, qk32, ALU.mult, [B_qk32], [B_sq])
                        red(ssq[:, 0:10], sq.rearrange("p (h d) -> p h d", d=64), [B_sq], [B_ssq])
                        act(lnq[:, 0:10], ssq[:, 0:10], AF.Ln, [B_ssq], [B_lnq], bias=EPS, scale=1.0 / 64)
                        act(rq[:, 0:10], lnq[:, 0:10], AF.Exp, [B_lnq], [B_rq], scale=-0.5)
                        qk3 = qk32.rearrange("p (h d) -> p h d", d=64)
                        tt("vector", qk3, qk3, rq[:, 0:10].unsqueeze(2).to_broadcast([128, 10, 64]), ALU.mult, [B_qk32, B_rq], [B_qk32])
                        tt("vector", qk3[:, 0:8, :], qk3[:, 0:8, :], lay(l, L_QG, 64).unsqueeze(1).to_broadcast([128, 8, 64]),
                           ALU.mult, [B_qk32, B_cst], [B_qk32])
                        tt("vector", qk3[:, 8:10, :], qk3[:, 8:10, :], lay(l, L_KG, 64).unsqueeze(1).to_broadcast([128, 2, 64]),
                           ALU.mult, [B_qk32, B_cst], [B_qk32])
                        kd4 = kd_bf.rearrange("p (h u d) -> p h u d", h=2, u=2)
                        if c >= 1:
                            lat_tile = tt_i - 2
                            t13 = t1.rearrange("p (h d) -> p h d", d=64)
                            tt("vector", t13, qk3, cos_t[:, lat_tile, :].unsqueeze(1).to_broadcast([128, 10, 64]), ALU.mult,
                               [B_qk32, B_cst], [B_t1])
                            q5 = qk32.rearrange("p (h a f q) -> p h a f q", h=10, a=2, f=2)
                            t25 = t2.rearrange("p (h a f q) -> p h a f q", h=10, a=2, f=2)
                            s4 = sin_t[:, lat_tile, :].rearrange("p (a f q) -> p a f q", a=2, f=2)
                            for f in range(2):
                                tt("vector", t25[:, :, :, f, :], q5[:, :, :, 1 - f, :],
                                   s4[:, :, f, :].unsqueeze(1).to_broadcast([128, 10, 2, 16]), ALU.mult, [B_qk32, B_cst], [B_t2])
                            tt("vector", q_bf, t1[:, 0:512], t2[:, 0:512], ALU.add, [B_t1, B_t2], [B_qbf])
                            t1k = t1[:, 512:640].rearrange("p (h d) -> p h d", d=64).unsqueeze(2).to_broadcast([128, 2, 2, 64])
                            t2k = t2[:, 512:640].rearrange("p (h d) -> p h d", d=64).unsqueeze(2).to_broadcast([128, 2, 2, 64])
                            tt("vector", kd4, t1k, t2k, ALU.add, [B_t1, B_t2], [B_kdbf])
                        else:
                            cp("vector", q_bf, qk32[:, 0:512], [B_qk32], [B_qbf])
                            cp("vector", kd4, qk32[:, 512:640].rearrange("p (h d) -> p h d", d=64).unsqueeze(2).to_broadcast([128, 2, 2, 64]),
                               [B_qk32], [B_kdbf])
                        if bi == 0 and l == 0 and tt_i == 2:
                            dump("qk32", lambda: qk32, [B_qk32])
                        def late(q_bf=q_bf, kd_bf=kd_bf, B_qbf=B_qbf, B_kdbf=B_kdbf, tt_i=tt_i, c=c):
                            pT = psb(7)
                            for j in range(4):
                                tr(pT[:, j * 128:(j + 1) * 128], q_bf[:, j * 128:(j + 1) * 128], ident_bf, [B_qbf, B_ident], [PB[7]])
                            for kv in range(2):
                                tr(pT[:, 512 + kv * 128:512 + (kv + 1) * 128], kd_bf[:, kv * 128:(kv + 1) * 128], ident_bf,
                                   [B_kdbf, B_ident], [PB[7]])
                            tsl = slice(tt_i * 128, (tt_i + 1) * 128)
                            act(qT[:, :, tsl], pT[:, 0:512].rearrange("p (j n) -> p j n", j=4), AF.Copy, [PB[7]], [B_qT[c]])
                            for u_ in range(2):
                                act(kTz[u_ * 64:(u_ + 1) * 64, :, u_, tsl], pT[u_ * 64:(u_ + 1) * 64, 512:768].rearrange("p (j n) -> p j n", j=2),
                                    AF.Copy, [PB[7]], [B_kTd[tt_i]])
                        late_q.append(late)
                        act(Vaug[:, tt_i, :, 0:64], ps[bB][:, 128:256].rearrange("p (h d) -> p h d", d=64), AF.Copy, [PB[bB]], [B_V[tt_i]])
                        act(mk_tok[:, tt_i, :], ps[bC][:, 0:256], AF.Copy, [PB[bC]], [B_mk[tt_i]], scale=0.125)
                        cp("vector", mVaug[:, tt_i, :, 0:64], ps[bC][:, 256:512].rearrange("p (h d) -> p h d", d=64), [PB[bC]], [B_mV[tt_i]])
                        tt("vector", G[:, tt_i, :], ps[bB][:, 256:272], lay(l, L_BG, 16), ALU.add, [PB[bB], B_cst], [B_G[tt_i]])
                        gf = G[:, tt_i, :].rearrange("p (d g) -> p d g", d=2)[:, :, 4:8]
                        act(gf, gf, AF.Exp, [B_G[tt_i]], [B_G[tt_i]], scale=-1.0)
                        act(gf, gf, AF.Ln, [B_G[tt_i]], [B_G[tt_i]], bias=1.0)
                    til = tiles_of(c)
                    upos = 8 if c == 0 else off_c + 24
                    fm = [(768, "mq", 0), (896, "mq", 1), (1024, "mk", 0), (1152, "mk", 1),
                          (1536, "mo", 0), (1664, "mo", 1), (1808, "pp", 0), (1936, "pp", 1)]
                    for (c0, kind, j) in fm:
                        bank = 1 + fm_rot % 6
                        fm_rot += 1
                        if c == 4 and late_q:
                            late_q.pop(0)()
                        for kc in range(8):
                            mm(ps[bank][:, 0:n], win[:, kc, c0:c0 + 128], hT[:, kc, 0:n], kc == 0, kc == 7, [B_win, B_hT], [PB[bank]])
                        if kind == "mq":
                            act(mqT[:, j, off_c:off_c + n], ps[bank][:, 0:n], AF.Copy, [PB[bank]], [B_mqT[t] for t in til])
                        elif kind == "mk":
                            act(mkT[:, j, off_c:off_c + n], ps[bank][:, 0:n], AF.Copy, [PB[bank]], [B_mkT[t] for t in til], scale=0.125)
                        elif kind == "mo":
                            act(moT[:, j, off_c:off_c + n], ps[bank][:, 0:n], AF.Sigmoid, [PB[bank]], [B_moT[c]])
                        else:
                            cp("vector", uT[:, j, upos:upos + n], ps[bank][:, 0:n], [PB[bank]], [B_uT[c]])
                while late_q:
                    late_q.pop(0)()
                if bi == 0 and l == 0:
                    dump("qT", lambda: qT[:, :, 256:768], B_qT)
                    dump("kTd", lambda: kTz[:, :, 0, 256:768], B_kTd)
                    dump("G", lambda: G.rearrange("p t c -> p (t c)"), B_G)
                S.barrier()
                stop('A')

                o = OV0
                Hsum = R(o, NT * 256).rearrange("p (t h d) -> p t h d", t=NT, h=4); o += NT * 256
                PT = [R(o, 512, BF16), R(o + 256, 512, BF16)]; o += 512
                att_tok = R(o, 4 * 512, BF16).rearrange("p (q f) -> p q f", q=4); o += 1024
                rinv = R(o, 8); o += 8
                CSs = R(o, NT * 16).rearrange("p (t c) -> p t c", t=NT); o += NT * 16
                biasD = R(o, NT * 8).rearrange("p (t c) -> p t c", t=NT); o += NT * 8
                wkarg = R(o, NT * 8).rearrange("p (t c) -> p t c", t=NT); o += NT * 8
                wk = R(o, NT * 8).rearrange("p (t c) -> p t c", t=NT); o += NT * 8
                dstate = R(o, NT * 8).rearrange("p (t c) -> p t c", t=NT); o += NT * 8
                eb = R(o, NT * 8).rearrange("p (t c) -> p t c", t=NT); o += NT * 8
                LFhi = [R(o + i * 256, 512, BF16).rearrange("p (h n) -> p h n", h=4) for i in range(2)]; o += 512
                LFlo = [R(o + i * 256, 512, BF16).rearrange("p (h n) -> p h n", h=4) for i in range(2)]; o += 512
                Ghi = R(o, NT * 16, BF16).rearrange("p (t c) -> p t c", t=NT); o += NT * 8
                Glo = R(o, NT * 16, BF16).rearrange("p (t c) -> p t c", t=NT); o += NT * 8
                B_Ghl = Buf("Ghl")
                DT = [R(o, 512).rearrange("p (h n) -> p h n", h=4), R(o + 512, 512).rearrange("p (h n) -> p h n", h=4)]; o += 1024
                PTm = [R(o, 512, BF16).rearrange("p (h n) -> p h n", h=4), R(o + 256, 512, BF16).rearrange("p (h n) -> p h n", h=4)]; o += 512
                tmpO = R(o, 260).rearrange("p (h c) -> p h c", h=4); o += 260
                Ocomb = R(o, 260).rearrange("p (h c) -> p h c", h=4); o += 260
                dd = R(o, 4); o += 4
                rr = R(o, 4); o += 4
                hdir = R(o, 256).rearrange("p (h d) -> p h d", h=4); o += 256
                mkw = R(o, 256, BF16); o += 128
                sqh = R(o, 256).rearrange("p (h d) -> p h d", h=4); o += 256
                ssh = R(o, 4); o += 4
                lnh = R(o, 4); o += 4
                rh = R(o, 4); o += 4
                hn = R(o, 256, BF16); o += 128
                PW = 544
                z32 = R(o, 2 * PW).rearrange("p (j n) -> p j n", j=2); o += 2 * PW
                pX = R(o, 2 * PW).rearrange("p (j n) -> p j n", j=2); o += 2 * PW
                pY = R(o, 2 * PW).rearrange("p (j n) -> p j n", j=2); o += 2 * PW
                ppT = R(o, 2 * 512, BF16).rearrange("p (j n) -> p j n", j=2); o += 512
                etmp = R(o, 8); o += 8
                assert o <= ARENA_W
                B_H = [Buf("H%d" % t) for t in range(NT)]
                B_PT = [Buf("PT0"), Buf("PT1")]
                B_att, B_rinv, B_CSs, B_bD, B_wka, B_wk, B_ds, B_eb = (Buf(n) for n in ["att", "rinv", "CSs", "bD", "wka", "wk", "ds", "eb"])
                B_LF = [Buf("LF0"), Buf("LF1")]
                B_DT = [Buf("DT0"), Buf("DT1")]
                B_PTm = [Buf("PTm0"), Buf("PTm1")]
                B_tmpO, B_Oc, B_dd, B_rr, B_hdir, B_mkw, B_sqh, B_ssh, B_lnh, B_rh, B_hn = (Buf(n) for n in [
                    "tmpO", "Oc", "dd", "rr", "hdir", "mkw", "sqh", "ssh", "lnh", "rh", "hn"])
                B_z, B_pX, B_pY, B_ppT, B_et = (Buf(n) for n in ["z", "pX", "pY", "ppT", "et"])

                CSp = ps[5][:, 0:NT * 16].rearrange("p (t c) -> p t c", t=NT)
                cp("vector", Ghi, G, B_G, [B_Ghl])
                tt("vector", Glo, G, Ghi, ALU.subtract, B_G + [B_Ghl], [B_Ghl])
                for t_i in range(NT):
                    for (c0_, lw, g0_) in [(0, negU[0], 4), (4, negU[1], 12), (8, negones_b, 4), (12, negones_b, 12)]:
                        mm(CSp[:, t_i, c0_:c0_ + 4], lw, Ghi[:, t_i, g0_:g0_ + 4], True, False, [B_cst, B_Ghl], [PB[5]])
                        mm(CSp[:, t_i, c0_:c0_ + 4], lw, Glo[:, t_i, g0_:g0_ + 4], False, True, [B_cst, B_Ghl], [PB[5]])
                cp("vector", CSs, CSp, [PB[5]], [B_CSs])
                for d_ in range(2):
                    tt("vector", biasD[:, :, d_ * 4:d_ * 4 + 4], G[:, :, d_ * 8:d_ * 8 + 4], CSs[:, :, d_ * 4:d_ * 4 + 4], ALU.subtract,
                       B_G + [B_CSs], [B_bD])
                tt("vector", wkarg, biasD, CSs[:, :, 8:16], ALU.add, [B_bD, B_CSs], [B_wka])
                act(wk, wkarg, AF.Exp, [B_wka], [B_wk])
                act(dstate, CSs[:, :, 8:16], AF.Exp, [B_CSs], [B_ds])
                act(eb, CSs[:, :, 0:8], AF.Exp, [B_CSs], [B_eb])

                def ml_stage0(d, t_i, pp, want_out):
                    cp("vector", LFhi[pp], Ghi[:, t_i, d * 8 + 4:d * 8 + 8].unsqueeze(2).to_broadcast([128, 4, 128]), [B_Ghl], [B_LF[pp]])
                    cp("vector", LFlo[pp], Glo[:, t_i, d * 8 + 4:d * 8 + 8].unsqueeze(2).to_broadcast([128, 4, 128]), [B_Ghl], [B_LF[pp]])

                def ml_stage1(d, t_i, pp, want_out):
                    if not want_out:
                        return
                    tsl = slice(t_i * 128, (t_i + 1) * 128)
                    Bm = ps[5][:, :].rearrange("p (h n) -> p h n", h=4)
                    for h in range(4):
                        mm(Bm[:, h, :], ident_bf, Mn[d], True, False, [B_cst, B_ident], [PB[5]])
                        mm(Bm[:, h, :], LFhi[pp][:, h, :], negU[d], False, False, [B_LF[pp], B_cst], [PB[5]])
                        mm(Bm[:, h, :], LFlo[pp][:, h, :], negU[d], False, True, [B_LF[pp], B_cst], [PB[5]])
                    for h in range(4):
                        act(DT[pp][:, h, :], Bm[:, h, :], AF.Exp, [PB[5], B_bD], [B_DT[pp]], bias=biasD[:, t_i, d * 4 + h:d * 4 + h + 1])
                    STb = [ps[6][:, 0:256].rearrange("p (j n) -> p j n", j=2), ps[7][:, 0:256].rearrange("p (j n) -> p j n", j=2)]
                    SBk = [PB[6], PB[7]]
                    for blk in range(2):
                        base = blk * 64
                        for j in range(2):
                            mm(STb[blk][:, j, :], mkT[base:base + 64, j, tsl], mqT[base:base + 64, j, tsl], True, True,
                               [B_mkT[t_i], B_mqT[t_i]], [SBk[blk]])
                    for blk in range(2):
                        tt("vector", PTm[pp].rearrange("p (j b) n -> p j b n", b=2)[:, :, blk, :], STb[blk],
                           DT[pp].rearrange("p (j b) n -> p j b n", b=2)[:, :, blk, :], ALU.mult, [SBk[blk], B_DT[pp]], [B_PTm[pp]])

                def ml_stage2(d, t_i, pp, want_out):
                    tsl = slice(t_i * 128, (t_i + 1) * 128)
                    if want_out:
                        OAb = [ps[6][:, 256:386].rearrange("p (h c) -> p h c", h=2), ps[7][:, 256:386].rearrange("p (h c) -> p h c", h=2)]
                        OB = ps[4][:, 0:260].rearrange("p (h c) -> p h c", h=4)
                        for h in range(4):
                            mm(OAb[h // 2][:, h % 2, :], PTm[pp][:, h, :], mVaug[:, t_i, h, :], True, True, [B_PTm[pp], B_mV[t_i]], [PB[6 + h // 2]])
                        for j in range(2):
                            mm(ps[4][:, j * 130:(j + 1) * 130], mqT[:, j, tsl], CTb[:, j, :], True, True, [B_mqT[t_i], B_CTb], [PB[4]])
                        tt("vector", tmpO, OB, eb[:, t_i, d * 4:d * 4 + 4].unsqueeze(2).to_broadcast([128, 4, 65]), ALU.mult,
                           [PB[4], B_eb], [B_tmpO])
                        for hp in range(2):
                            tt("vector", Ocomb[:, 2 * hp:2 * hp + 2, :], tmpO[:, 2 * hp:2 * hp + 2, :], OAb[hp], ALU.add,
                               [B_tmpO, PB[6 + hp]], [B_Oc])
                        stt("vector", dd, Ocomb[:, :, 64], -1.0, Ocomb[:, :, 64], ALU.mult, ALU.max, [B_Oc], [B_dd])
                        ts("vector", dd, dd, 1.0, None, ALU.max, None, [B_dd], [B_dd])
                        recip(rr, dd, [B_dd], [B_rr])
                        if d == 0:
                            tt("vector", Hsum[:, t_i], Ocomb[:, :, 0:64], rr.unsqueeze(2).to_broadcast([128, 4, 64]), ALU.mult,
                               [B_Oc, B_rr], [B_H[t_i]])
                        else:
                            tt("vector", hdir, Ocomb[:, :, 0:64], rr.unsqueeze(2).to_broadcast([128, 4, 64]), ALU.mult,
                               [B_Oc, B_rr], [B_hdir])
                            tt("vector", Hsum[:, t_i], Hsum[:, t_i], hdir, ALU.add, [B_H[t_i], B_hdir], [B_H[t_i]])
                    tt("vector", mkw.rearrange("p (h d) -> p h d", h=4), mk_tok[:, t_i, :].rearrange("p (h d) -> p h d", h=4),
                       wk[:, t_i, d * 4:d * 4 + 4].unsqueeze(2).to_broadcast([128, 4, 64]), ALU.mult, [B_mk[t_i], B_wk], [B_mkw])

                def ml_stage3(d, t_i, pp, want_out):
                    for j in range(2):
                        mm(ps[3][:, j * 130:(j + 1) * 130], mkw[:, j * 128:(j + 1) * 128],
                           mVaug[:, t_i, 2 * j:2 * j + 2, :].rearrange("p h c -> p (h c)"), True, True, [B_mkw, B_mV[t_i]], [PB[3]])
                    for j in range(2):
                        for blk in range(2):
                            rows = slice(blk * 64, (blk + 1) * 64)
                            cols = slice(blk * 65, (blk + 1) * 65)
                            hh = 2 * j + blk
                            stt("vector", Cst[rows, j, cols], Cst[rows, j, cols], dstate[rows, t_i, d * 4 + hh:d * 4 + hh + 1],
                                ps[3][rows, j * 130 + blk * 65:j * 130 + (blk + 1) * 65], ALU.mult, ALU.add,
                                [B_C, B_ds, PB[3]], [B_C])
                            cp("vector", CTb[rows, j, cols], Cst[rows, j, cols], [B_C], [B_CTb])

                def ml_reset(d):
                    mset("vector", Cst.rearrange("p j n -> p (j n)"), 0.0, [B_C])
                    mset("vector", CTb.rearrange("p j n -> p (j n)"), 0.0, [B_CTb])

                def ml_readout(t_i):
                    tsl = slice(t_i * 128, (t_i + 1) * 128)
                    cch = 0 if t_i < 2 else 1 + (t_i - 2) // 4
                    tt("vector", sqh, Hsum[:, t_i], Hsum[:, t_i], ALU.mult, [B_H[t_i]], [B_sqh])
                    red(ssh, sqh, [B_sqh], [B_ssh])
                    act(lnh, ssh, AF.Ln, [B_ssh], [B_lnh], bias=EPS, scale=1.0 / 64)
                    act(rh, lnh, AF.Exp, [B_lnh], [B_rh], scale=-0.5)
                    tt("vector", hn.rearrange("p (h d) -> p h d", h=4), Hsum[:, t_i], rh.unsqueeze(2).to_broadcast([128, 4, 64]), ALU.mult,
                       [B_H[t_i], B_rh], [B_hn])

                def ml_readout2(t_i):
                    tsl = slice(t_i * 128, (t_i + 1) * 128)
                    cch = 0 if t_i < 2 else 1 + (t_i - 2) // 4
                    pT = psb(3)
                    for j in range(2):
                        tr(pT[:, j * 128:(j + 1) * 128], hn[:, j * 128:(j + 1) * 128], ident_bf, [B_hn, B_ident], [PB[3]])
                    for j in range(2):
                        stt("vector", mqT[:, j, tsl], pT[:, j * 128:(j + 1) * 128], lay(l, L_MLG, 2)[:, j:j + 1], moT[:, j, tsl],
                            ALU.mult, ALU.mult, [PB[3], B_cst, B_moT[cch]], [B_mqT[t_i]])

                steps_l = []
                step = 0
                for d in range(2):
                    order = list(range(NT)) if d == 0 else [1, 0] + list(range(NT - 1, 1, -1))
                    for t_i in order:
                        want_out = (l == 0) or (t_i >= 2)
                        steps_l.append((d, t_i, step % 2, want_out))
                        step += 1
                stages = []
                ro_pending = []
                for k_, args in enumerate(steps_l):
                    d, t_i, pp_, want_out = args
                    if k_ == 0 or steps_l[k_ - 1][0] != d:
                        stages.append((ml_reset, (d,)))
                    if k_ == 0 and want_out:
                        stages.append((ml_stage0, args))
                    if k_ + 1 < len(steps_l) and steps_l[k_ + 1][3]:
                        stages.append((ml_stage0, steps_l[k_ + 1]))
                    if want_out:
                        stages.append((ml_stage1, args))
                    stages.append((ml_stage2, args))
                    if ro_pending:
                        stages.append((ml_readout2, (ro_pending.pop(0),)))
                    stages.append((ml_stage3, args))
                    if d == 1 and want_out:
                        stages.append((ml_readout, (t_i,)))
                        ro_pending.append(t_i)
                while ro_pending:
                    stages.append((ml_readout2, (ro_pending.pop(0),)))
                if bi == 0 and l == 0 and ada_todo:
                    wa2 = [R(WS_OFF + i * 4096, 8 * 1024, BF16).rearrange("p (k n) -> p k n", k=8) for i in range(2)]
                    B_wa2 = [Buf("wa2_0"), Buf("wa2_1")]

                    todo_ = list(ada_todo)

                    def ada_stage_dma(idx):
                        lw, piece = todo_[idx]
                        ada_dma(lw, piece, wa2[idx % 2], B_wa2[idx % 2])

                    def ada_stage_mm(idx):
                        lw, piece = todo_[idx]
                        ada_mm(lw, piece, wa2[idx % 2], B_wa2[idx % 2], 4, 300)

                    ins = []
                    sp_ = max(10, (len(stages) - 8) // (len(ada_todo) + 1))
                    for idx in range(len(ada_todo)):
                        ins.append(((idx + 1) if idx < 2 else sp_ * (idx - 1) + 5, (ada_stage_dma, (idx,))))
                        ins.append((sp_ * (idx + 1) + 4, (ada_stage_mm, (idx,))))
                    ins.sort(key=lambda x: x[0])
                    merged = []
                    ii = 0
                    for pos_, st_ in enumerate(stages):
                        while ii < len(ins) and ins[ii][0] <= pos_:
                            merged.append(ins[ii][1])
                            ii += 1
                        merged.append(st_)
                    while ii < len(ins):
                        merged.append(ins[ii][1])
                        ii += 1
                    stages = merged
                    ada_todo = []
                if l == 0 and fin_todo:
                    bi_f = fin_todo.pop(0)
                    fo = WS_OFF
                    xcq = R(fo, 8 * 512).rearrange("p (k n) -> p k n", k=8); fo += 4096
                    sqq = [R(fo + i * 256, 512, BF16) for i in range(2)]; fo += 512
                    lnq_ = R(fo, 512); fo += 512
                    rsq_ = R(fo, 512); fo += 512
                    B_xcq, B_lnq_, B_rsq_ = Buf("xcq"), Buf("lnq_"), Buf("rsq_")
                    B_sqq = [Buf("sqq0"), Buf("sqq1")]

                    def fin_stage2(c, bi_f=bi_f):
                        off_c, n = CH[c]
                        sqf = R(WS_OFF + 5632, 8 * 512, BF16).rearrange("p (k n) -> p k n", k=8)
                        B_sqf = B_sqq[0]
                        dma("sync", xcq[:, :, 0:n], xs[bi_f].rearrange("k p n -> p k n")[:, :, off_c:off_c + n], [B_xs[bi_f][c]], [B_xcq])
                        for kc in range(8):
                            tt("gpsimd", sqf[:, kc, 0:n], xcq[:, kc, 0:n], xcq[:, kc, 0:n], ALU.mult, [B_xcq], [B_sqf])
                        for q_ in range(4):
                            for kc in range(8):
                                mm(ps[4][:, 300:428], ones_bf, sqf[:, kc, q_ * 128:(q_ + 1) * 128], kc == 0, kc == 7,
                                   [B_ones, B_sqf], [PB[4]])
                            act(lnq_[:, q_ * 128:(q_ + 1) * 128], ps[4][:, 300:428], AF.Ln, [PB[4]], [B_lnq_], bias=EPS, scale=1.0 / D)
                        act(rsq_[:, 0:n], lnq_[:, 0:n], AF.Exp, [B_lnq_], [B_rsq_], scale=-0.5)
                        for kc in range(8):
                            stt("vector", xcq[:, kc, 0:n], xcq[:, kc, 0:n], cst[:, C_FN + kc:C_FN + kc + 1], rsq_[:, 0:n], ALU.mult, ALU.mult,
                                [B_xcq, B_cst, B_rsq_], [B_xcq])
                        b_ = Buf("y%d_%d" % (bi_f, c))
                        out_bufs.append(b_)
                        dma("sync", yT[bi_f].rearrange("k p n -> p k n")[:, :, off_c - 256:off_c - 256 + n], xcq[:, :, 0:n], [B_xcq], [b_])

                    merged = []
                    gap_ = max(1, len(stages) // 5)
                    for pos_, st_ in enumerate(stages):
                        if pos_ % gap_ == gap_ // 2 and 1 + pos_ // gap_ <= 4:
                            merged.append((fin_stage2, (1 + pos_ // gap_,)))
                        merged.append(st_)
                    stages = merged
                stage_pos = [0]

                def pump(n=1):
                    for _ in range(n):
                        if stage_pos[0] < len(stages):
                            f_, a_ = stages[stage_pos[0]]
                            stage_pos[0] += 1
                            f_(*a_)
                            if f_ is ml_reset:
                                pump(1)

                n_steps_total = sum((2 if c == 0 else NT) * 8 for c in chunks)
                n_stage = len(stages)
                done_steps = 0
                for c in chunks:
                    off_c, n = CH[c]
                    nqt = n // 128
                    keyt = [0, 1] if c == 0 else list(range(NT))
                    steps = [(h, ki) for h in range(8) for ki in range(len(keyt))]

                    def emit_st(si):
                        h, ki = steps[si]
                        j, base, kvh = h // 2, (h % 2) * 64, h // 4
                        kt = keyt[ki]
                        sb = si % 2
                        mm(ps[sb][:, 0:n], kTz[:, kvh, h % 2, kt * 128:(kt + 1) * 128], qT[:, j, off_c:off_c + n],
                           True, True, [B_kTd[kt], B_qT[c]], [PB[sb]])

                    emit_st(0)
                    for si, (h, ki) in enumerate(steps):
                        kvh = h // 4
                        kt = keyt[ki]
                        sb = si % 2
                        if si + 1 < len(steps):
                            emit_st(si + 1)
                        act(PT[sb][:, 0:n], ps[sb][:, 0:n], AF.Exp, [PB[sb]], [B_PT[sb]], scale=0.125)
                        for qt in range(nqt):
                            mm(ps[2][:, qt * 65:(qt + 1) * 65], PT[sb][:, qt * 128:(qt + 1) * 128], Vaug[:, kt, kvh, :],
                               ki == 0 and qt == 0, ki == len(keyt) - 1, [B_PT[sb], B_V[kt]], [PB[2]], skip=True)
                        if ki == len(keyt) - 1:
                            O3 = ps[2][:, 0:nqt * 65].rearrange("p (q c) -> p q c", c=65)
                            recip(rinv[:, 0:nqt], O3[:, :, 64], [PB[2]], [B_rinv])
                            tt("vector", att_tok[:, 0:nqt, h * 64:(h + 1) * 64], O3[:, :, 0:64],
                               rinv[:, 0:nqt].unsqueeze(2).to_broadcast([128, nqt, 64]), ALU.mult, [PB[2], B_rinv], [B_att])
                        done_steps += 1
                        target = (done_steps * n_stage) // n_steps_total
                        if target > stage_pos[0] and (done_steps % 3 == 0 or c == 0):
                            pump(1)
                    if bi == 0 and l == 0 and c == 1:
                        dump("att_tok", lambda: att_tok[:, 0, :], [B_att])
                    pT = psb(0)
                    for qt in range(nqt):
                        for jj in range(4):
                            tr(pT[:, jj * 128:(jj + 1) * 128], att_tok[:, qt, jj * 128:(jj + 1) * 128], ident_bf, [B_att, B_ident], [PB[0]])
                        act(qT[:, :, off_c + qt * 128:off_c + (qt + 1) * 128], pT[:, 0:512].rearrange("p (j n) -> p j n", j=4), AF.Copy,
                            [PB[0]], [B_qT[c]])
                pump(len(stages))
                if bi == 0 and l == 0:
                    dump("Hsum", lambda: Hsum[:, 2].rearrange("p h d -> p (h d)"), [B_H[2]])
                stop('B2')
                wo = R(OV0, 8 * 1024, BF16).rearrange("p (k n) -> p k n", k=8)
                B_wo = Buf("wo")
                for kc in range(8):
                    dma("gpsimd", wo[:, kc, :], w_out[l].rearrange("(k p) n -> p k n", p=128)[:, kc, :], [], [B_wo] + B_H)
                for c in chunks:
                    off_c, n = CH[c]
                    W = n + 16
                    upos = 8 if c == 0 else off_c + 24
                    til = tiles_of(c)
                    cp("vector", z32[:, :, 0:W], uT[:, :, upos - 8:upos + n + 8], B_uT + [B_uTall], [B_z])
                    lvl_src = z32
                    bufs = [(pX, B_pX), (pY, B_pY)]
                    B_src_l = B_z
                    for lv in range(4):
                        sh = 1 << lv if lv > 0 else 1
                        dst, B_dst = bufs[lv % 2]
                        if lv == 0:
                            tt("vector", dst[:, :, 1:W], z32[:, :, 1:W], z32[:, :, 0:W - 1], ALU.add, [B_z], [B_dst])
                        else:
                            s = 1 << (lv - 1)
                            tt("vector", dst[:, :, s:W - s], lvl_src[:, :, 0:W - 2 * s], lvl_src[:, :, 2 * s:W], ALU.add, [B_src_l], [B_dst])
                        lvl_src, B_src_l = dst, B_dst
                        w = 2 << lv
                        jg, rows = lv // 2, slice((lv % 2) * 64, (lv % 2) * 64 + 64)
                        stt("vector", ppT[rows, jg, 0:n], dst[rows, jg, 8:8 + n], 1.0 / w, z32[rows, jg, 8:8 + n], ALU.mult, ALU.subtract,
                            [B_dst, B_z], [B_ppT])
                        edges = []
                        if c in (0, 1):
                            edges.append((0, 0))
                        if c in (0, 4):
                            edges.append((1, n - 8))
                        for (side, e0) in edges:
                            ec = cst[rows, C_EDGE + jg * 16 + side * 8:C_EDGE + jg * 16 + side * 8 + 8]
                            tt("vector", etmp[rows, :], dst[rows, jg, 8 + e0:16 + e0], ec, ALU.mult, [B_dst, B_cst], [B_et])
                            tt("vector", ppT[rows, jg, e0:e0 + 8], etmp[rows, :], z32[rows, jg, 8 + e0:16 + e0], ALU.subtract,
                               [B_et, B_z], [B_ppT])
                    for j in range(2):
                        mm(ps[3][:, 0:n], pw_sb[:, l, j, :], ppT[:, j, 0:n], True, True, [B_pw, B_ppT], [PB[3]])
                        ts("vector", mkT[:, j, off_c:off_c + n], ps[3][:, 0:n], lay(l, L_PSC, 2)[:, j:j + 1], None, ALU.mult, None,
                           [PB[3], B_cst], [B_mkT[t] for t in til])
                if bi == 0 and l == 0:
                    dump("mix", lambda: qT[:, :, 256:384], B_qT)
                    dump("mixml", lambda: mqT[:, :, 256:384], B_mqT)
                    dump("mixpool", lambda: mkT[:, :, 256:384], B_mkT)
                S.barrier()
                stop('B')

                H2_OFF = WS_OFF
                h2T = R(H2_OFF, 8 * NTOK, BF16).rearrange("p (k n) -> p k n", k=8)
                B_h2T = [Buf("h2T%d" % c) for c in range(5)]
                o = OV0
                wo = R(o, 8 * 1024, BF16).rearrange("p (k n) -> p k n", k=8); o += 4096
                xcC = [R(o + i * 4096, 8 * 512).rearrange("p (k n) -> p k n", k=8) for i in range(2)]; o += 8192
                sqr = [R(o + i * 256, 512, BF16) for i in range(2)]; o += 512
                rstd = R(o, 512); o += 512
                tmpAA = [R(o + i * 512, 512) for i in range(2)]; o += 1024
                lnr = tmpAA[1]
                assert o <= H2_OFF, (o, H2_OFF)
                B_lnr, B_rstd = Buf("lnr"), Buf("rstd")
                B_xcC = [Buf("xcC0"), Buf("xcC1")]
                B_sqr = [Buf("sqr0"), Buf("sqr1")]
                B_tmpAA = [Buf("tmpA0"), Buf("tmpA1")]
                wgs = [R(AB0 + i * 2048, 8 * 256, BF16).rearrange("p (k n) -> p k n", k=8) for i in range(2)]
                wus = [R(AB0 + 1024 + i * 2048, 8 * 256, BF16).rearrange("p (k n) -> p k n", k=8) for i in range(2)]
                B_wgs = [Buf("wg0"), Buf("wg1")]
                B_wus = [Buf("wu0"), Buf("wu1")]

                def load_slab(f):
                    s_ = (f // 2) % 2
                    dma("gpsimd", wgs[s_], w_g[l].rearrange("(k p) n -> p k n", p=128)[:, :, f * 128:(f + 2) * 128], [], [B_wgs[s_]])
                    dma("gpsimd", wus[s_], w_u[l].rearrange("(k p) n -> p k n", p=128)[:, :, f * 128:(f + 2) * 128], [], [B_wus[s_]])

                load_slab(0)
                load_slab(2)
                bankc = [0]

                def loadC(i):
                    c = chunks[i]
                    off_c, n = CH[c]
                    dma("sync", xcC[i % 2][:, :, 0:n], src[bi].rearrange("k p n -> p k n")[:, :, off_c:off_c + n], [B_src[c]], [B_xcC[i % 2]])

                def projC(i):
                    c = chunks[i]
                    off_c, n = CH[c]
                    r = 2 if c == 0 else bi
                    til = tiles_of(c)
                    xc, B_xc = xcC[i % 2], B_xcC[i % 2]
                    mix_reads = [B_qT[c]] + [B_mqT[t] for t in til] + [B_mkT[t] for t in til]
                    for oc in range(8):
                        bank = bankc[0] % 4
                        bankc[0] += 1
                        for kc in range(8):
                            mm(ps[bank][:, 0:n], wo[:, kc, oc * 128:(oc + 1) * 128], mixT(kc)[:, off_c:off_c + n], kc == 0, kc == 7,
                               [B_wo] + mix_reads, [PB[bank]])
                        stt("vector", xc[:, oc, 0:n], ps[bank][:, 0:n], modcol(l, r, 16 + oc), xc[:, oc, 0:n], ALU.mult, ALU.add,
                            [PB[bank], B_mods, B_xc], [B_xc])
                    dma("sync", xs2.rearrange("k p n -> p k n")[:, :, off_c:off_c + n], xc[:, :, 0:n], [B_xc], [B_xs2[c]])
                    if bi == 0 and l == 0 and c == 1:
                        dump("x2", lambda: xc[:, 0, :], [B_xc])

                def normC(i):
                    c = chunks[i]
                    off_c, n = CH[c]
                    r = 2 if c == 0 else bi
                    xc, B_xc = xcC[i % 2], B_xcC[i % 2]
                    rms_rstd(xc, n, sqr, lnr, rstd, B_xc, B_sqr, B_lnr, B_rstd, 4)
                    for kc in range(8):
                        tA, B_tA = tmpAA[kc % 2], B_tmpAA[kc % 2]
                        stt("vector", tA[:, 0:n], xc[:, kc, 0:n], modcol(l, r, 32 + kc), rstd[:, 0:n], ALU.mult, ALU.mult,
                            [B_xc, B_mods, B_rstd], [B_tA])
                        act(h2T[:, kc, off_c:off_c + n], tA[:, 0:n], AF.Identity, [B_tA, B_mods], [B_h2T[c]], bias=modcol(l, r, 24 + kc))

                loadC(0)
                if len(chunks) > 1:
                    loadC(1)
                projC(0)
                for i in range(len(chunks)):
                    if i + 1 < len(chunks):
                        projC(i + 1)
                    normC(i)
                    if i + 2 < len(chunks):
                        loadC(i + 2)
                stop('C')

                aT = R(AB0 + 4096, NF * NTOK, BF16).rearrange("p (f n) -> p f n", f=NF)
                o = AB0 + 4096 + NF * NTOK // 2
                sg = [R(o, 512), R(o + 512, 512)]; o += 1024
                D2_OFF = o
                assert o <= H2_OFF
                B_aT = [[Buf("aT%d_%d" % (f, c)) for c in range(5)] for f in range(NF)]
                B_sg = [Buf("sg0"), Buf("sg1")]
                wds = [R(D2_OFF + i * (NF * 128), NF * 256, BF16).rearrange("p (f n) -> p f n", f=NF) for i in range(2)]
                B_wds = [Buf("wd0"), Buf("wd1")]

                def load_wd(oc, extra_r=()):
                    dma("gpsimd", wds[(oc // 2) % 2], w_d[l].rearrange("(f p) n -> p f n", p=128)[:, :, oc * 128:(oc + 2) * 128],
                        list(extra_r), [B_wds[(oc // 2) % 2]])

                k = 0
                for f in range(NF):
                    s_ = (f // 2) % 2
                    fo_ = (f % 2) * 128
                    if f % 2 == 0 and f >= 4:
                        load_slab(f)
                    if f == 16:
                        load_wd(0, B_xs2)
                        load_wd(2, B_xs2)
                    for c in chunks:
                        off_c, n = CH[c]
                        bg_, bu_ = (0, 1) if k % 2 == 0 else (2, 3)
                        sgi = k % 2
                        k += 1
                        for kc in range(8):
                            mm(ps[bg_][:, 0:n], wgs[s_][:, kc, fo_:fo_ + 128], h2T[:, kc, off_c:off_c + n], kc == 0, kc == 7, [B_wgs[s_], B_h2T[c]], [PB[bg_]])
                        for kc in range(8):
                            mm(ps[bu_][:, 0:n], wus[s_][:, kc, fo_:fo_ + 128], h2T[:, kc, off_c:off_c + n], kc == 0, kc == 7, [B_wus[s_], B_h2T[c]], [PB[bu_]])
                        act(sg[sgi][:, 0:n], ps[bg_][:, 0:n], AF.Silu, [PB[bg_]], [B_sg[sgi]])
                        tt("vector", aT[:, f, off_c:off_c + n], sg[sgi][:, 0:n], ps[bu_][:, 0:n], ALU.mult, [B_sg[sgi], PB[bu_]], [B_aT[f][c]])
                S.barrier(engs=["sync"])

                o = D2_OFF
                o += 2 * NF * 128
                x2o = [R(o, 512), R(o + 512, 512)]; o += 1024
                x3 = [R(o, 512), R(o + 512, 512)]; o += 1024
                assert o <= WS_OFF, (o, WS_OFF)
                B_x2o = [Buf("x2o0"), Buf("x2o1")]
                B_x3 = [Buf("x3_0"), Buf("x3_1")]
                k = 0
                nxt = (bi, l + 1) if l + 1 < n_layers else ((bi + 1, 0) if bi + 1 < n_b else None)
                for oc in range(8):
                    s_ = (oc // 2) % 2
                    oo_ = (oc % 2) * 128
                    if oc % 2 == 0 and oc >= 4:
                        load_wd(oc)
                    if oc == 2 and nxt is not None:
                        load_win(nxt[1], B_h2T)
                        win_state["loaded_for"] = nxt
                    for c in chunks:
                        off_c, n = CH[c]
                        r = 2 if c == 0 else bi
                        bank = k % 4
                        i2 = k % 2
                        k += 1
                        dma("sync", x2o[i2][:, 0:n], xs2[oc][:, off_c:off_c + n], [B_xs2[c]], [B_x2o[i2]])
                        for f in range(NF):
                            mm(ps[bank][:, 0:n], wds[s_][:, f, oo_:oo_ + 128], aT[:, f, off_c:off_c + n], f == 0, f == NF - 1, [B_wds[s_], B_aT[f][c]], [PB[bank]])
                        stt("vector", x3[i2][:, 0:n], ps[bank][:, 0:n], modcol(l, r, 40 + oc), x2o[i2][:, 0:n], ALU.mult, ALU.add,
                            [PB[bank], B_mods, B_x2o[i2]], [B_x3[i2]])
                        dma("sync", xs[bi][oc][:, off_c:off_c + n], x3[i2][:, 0:n], [B_x3[i2]], [B_xs[bi][c]])
                S.barrier()

            if bi == n_b - 1:
                o = OV0
                xcf = [R(o + i * 4096, 8 * 512).rearrange("p (k n) -> p k n", k=8) for i in range(2)]; o += 8192
                yof = [R(o + i * 4096, 8 * 512).rearrange("p (k n) -> p k n", k=8) for i in range(2)]; o += 8192
                sqr = [R(o + i * 256, 512, BF16) for i in range(2)]; o += 512
                lnr = R(o, 512); o += 512
                rstd = R(o, 512); o += 512
                B_xcf = [Buf("xcf0"), Buf("xcf1")]
                B_yof = [Buf("yof0"), Buf("yof1")]
                B_lnr, B_rstd = Buf("lnr"), Buf("rstd")
                B_sqr = [Buf("sqr0"), Buf("sqr1")]
                def load_f(c):
                    off_c, n = CH[c]
                    dma("sync", xcf[c % 2][:, :, 0:n], xs[bi].rearrange("k p n -> p k n")[:, :, off_c:off_c + n], [B_xs[bi][c]], [B_xcf[c % 2]])
                load_f(1)
                load_f(2)
                for c in range(1, 5):
                    off_c, n = CH[c]
                    xc, yo, B_xc, B_yo = xcf[c % 2], yof[c % 2], B_xcf[c % 2], B_yof[c % 2]
                    rms_rstd(xc, n, sqr, lnr, rstd, B_xc, B_sqr, B_lnr, B_rstd, c % 2)
                    for kc in range(8):
                        stt("vector", yo[:, kc, 0:n], xc[:, kc, 0:n], cst[:, C_FN + kc:C_FN + kc + 1], rstd[:, 0:n], ALU.mult, ALU.mult,
                            [B_xc, B_cst, B_rstd], [B_yo])
                    b = Buf("y%d_%d" % (bi, c))
                    out_bufs.append(b)
                    dma("sync", yT[bi].rearrange("k p n -> p k n")[:, :, off_c - 256:off_c - 256 + n], yo[:, :, 0:n], [B_yo], [b])
                    if c + 2 < 5:
                        load_f(c + 2)
                S.barrier()
            else:
                fin_todo.append(bi)
    except _Stop:
        pass

    S.op("sync", lambda e: e.nop(), out_bufs, [])
    S.emit(nc, st)
    st.close()
    return nc, S


def _rope_tables(rows=32, grid_w=64, head_dim=64, theta=10000.0):
    half = head_dim // 2
    inv_freq = (1.0 / (np.float32(theta) ** (np.arange(0, half, 2, dtype=np.float32) / np.float32(half)))).astype(np.float32)
    row = np.repeat(np.arange(rows, dtype=np.float32), grid_w)
    col = np.tile(np.arange(grid_w, dtype=np.float32), rows)
    a_row = row[:, None] * inv_freq
    a_col = col[:, None] * inv_freq
    ang = np.concatenate([a_row, a_row, a_col, a_col], -1).astype(np.float32)
    return np.cos(ang).astype(np.float32), np.sin(ang).astype(np.float32)


def _const_blob():
    blob = np.zeros((128, CW), np.float32)
    idx = np.arange(128)
    blob[:, C_IDENT:C_IDENT + 128] = np.eye(128, dtype=np.float32)
    blob[:, C_NEGUF:C_NEGUF + 128] = -(idx[:, None] <= idx[None, :]).astype(np.float32)
    blob[:, C_NEGUB:C_NEGUB + 128] = -(idx[:, None] >= idx[None, :]).astype(np.float32)
    blob[:, C_MNF:C_MNF + 128] = NEG * (idx[:, None] > idx[None, :]).astype(np.float32)
    blob[:, C_MNB:C_MNB + 128] = NEG * (idx[:, None] < idx[None, :]).astype(np.float32)
    blob[:, C_NEGONES:C_NEGONES + 128] = -1.0
    cos, sin = _rope_tables()
    sgn = np.ones(64, np.float32)
    sgn[0:16] = -1.0
    sgn[32:48] = -1.0
    sinS = sin * sgn[None, :]
    blob[:, C_COS:C_COS + 1024] = cos.reshape(16, 128, 64).transpose(1, 0, 2).reshape(128, 1024)
    blob[:, C_SIN:C_SIN + 1024] = sinS.reshape(16, 128, 64).transpose(1, 0, 2).reshape(128, 1024)
    wins = (2, 4, 8, 16)
    for p in range(128):
        for j in range(2):
            w = wins[2 * j + (1 if p >= 64 else 0)]
            for e in range(8):
                t = e
                cnt_l = (t + w // 2) - max(t - w // 2, 0)
                k = 8 - e
                cnt_r = min(w // 2, k) + w // 2
                blob[p, C_EDGE + j * 16 + e] = 1.0 / cnt_l
                blob[p, C_EDGE + j * 16 + 8 + e] = 1.0 / cnt_r
    return blob


def _fm(v, nchunk):
    return np.ascontiguousarray(np.asarray(v, np.float32).reshape(nchunk, 128).T)


_NC_CACHE = {}


def prep_in_maps(x, c, ctx, c_ctx, w_ada, b_ada, norm_mix, w_in, b_gates, q_norm, k_norm, ml_norm, pool_w, pool_scale, w_out,
                 norm_ffn, w_ffn_gate, w_ffn_up, w_ffn_down, final_norm):
    f = lambda a: np.ascontiguousarray(np.asarray(a, dtype=np.float32))
    x, c, ctx, c_ctx = f(x), f(c), f(ctx), f(c_ctx)
    n_cores = 8
    base = _const_blob()
    for l in range(2):
        o = C_LAY + l * LAYW
        base[:, o + L_BADA:o + L_BADA + 48] = _fm(b_ada[l], 48)
        base[:, o + L_NMIX:o + L_NMIX + 8] = _fm(norm_mix[l], 8)
        base[:, o + L_NFFN:o + L_NFFN + 8] = _fm(norm_ffn[l], 8)
        base[:, o + L_MLG:o + L_MLG + 2] = _fm(ml_norm[l], 2)
        base[:, o + L_PSC:o + L_PSC + 2] = _fm(pool_scale[l], 2)
        base[:, o + L_QG:o + L_QG + 64] = np.asarray(q_norm[l], np.float32)[None, :]
        base[:, o + L_KG:o + L_KG + 64] = np.asarray(k_norm[l], np.float32)[None, :]
        base[:, o + L_BG:o + L_BG + 16] = np.asarray(b_gates[l], np.float32)[None, :]
    base[:, C_FN:C_FN + 8] = _fm(final_norm, 8)
    pw = np.asarray(pool_w, np.float32)
    pwbd = np.zeros((2, 2, 128, 128), np.float32)
    for l in range(2):
        for j in range(2):
            pwbd[l, j, 0:64, 0:64] = pw[l, 2 * j]
            pwbd[l, j, 64:128, 64:128] = pw[l, 2 * j + 1]
    shared = {"pwbd": pwbd, "w_ada": f(w_ada), "w_in": f(w_in), "w_out": f(w_out), "w_g": f(w_ffn_gate), "w_u": f(w_ffn_up),
              "w_d": f(w_ffn_down)}
    in_maps = []
    for i in range(n_cores):
        seq = np.concatenate([ctx[2 * i:2 * i + 2], x[2 * i:2 * i + 2]], axis=1)
        xin = np.ascontiguousarray(seq.transpose(0, 2, 1)).reshape(2, 8, 128, NTOK)
        blob = base.copy()
        cs = np.stack([c[2 * i], c[2 * i + 1], c_ctx], axis=-1)
        blob[:, C_CSEL:C_CSEL + 24] = cs.reshape(8, 128, 3).transpose(1, 0, 2).reshape(128, 24)
        m = {"xin": xin, "cst": blob}
        m.update(shared)
        in_maps.append(m)
    return in_maps


def kernel(**inputs):
    n_cores = 8
    in_maps = prep_in_maps(**inputs)
    if "nc" not in _NC_CACHE:
        _NC_CACHE["nc"] = build_nc()[0]
    nc = _NC_CACHE["nc"]
    res = run_bass_kernel_spmd(nc, in_maps, core_ids=list(range(n_cores)))
    out = np.empty((16, 2048, 1024), np.float32)
    for i in range(n_cores):
        y = np.asarray(res.results[i]["yT"]).reshape(2, 1024, 2048)
        out[2 * i:2 * i + 2] = y.transpose(0, 2, 1)
    return out
```
